# Optimizing a Trainium2 kernel written in Bass

```python
import jax, jax.numpy as jnp
from jax import lax
import numpy as np

D_MODEL = 1024
BATCH = 4
SEQ = 4096
DEPTH = 1

CHUNK = 64
RET_HEADS = 4
RET_QK_DIM = 128
RET_V_DIM = 256
RET_QK = RET_HEADS * RET_QK_DIM
RET_V = RET_HEADS * RET_V_DIM
CONV_WIDTH = D_MODEL
CONV_K = 3
D_FF = 4 * D_MODEL
N_MOD = 6
ROPE_BASE = 10000.0
EPS = 1e-6
SPLITS = (RET_QK, RET_QK, RET_V, RET_V, CONV_WIDTH, CONV_WIDTH, CONV_WIDTH, D_MODEL, D_MODEL)
IN_WIDTH = sum(SPLITS)

kernel_name = "hybrid_retention_shortconv_block"


def _rmsnorm(t, w):
    tf = t.astype(jnp.float32)
    y = tf * lax.rsqrt(jnp.mean(tf * tf, axis=-1, keepdims=True) + EPS)
    return (y * w.astype(jnp.float32)).astype(t.dtype)


def _rotary(t, positions):
    d = t.shape[-1]
    inv_freq = ROPE_BASE ** (-jnp.arange(0, d, 2, dtype=jnp.float32) / d)
    ang = positions.astype(jnp.float32)[..., None] * inv_freq
    cos = jnp.cos(ang)[:, :, None, :]
    sin = jnp.sin(ang)[:, :, None, :]
    t1, t2 = t[..., : d // 2], t[..., d // 2:]
    return jnp.concatenate([t1 * cos - t2 * sin, t1 * sin + t2 * cos], axis=-1)


def _retention(q, k, v):
    B, S, H, dk = q.shape
    dv = v.shape[-1]
    nc = S // CHUNK
    log_g = jnp.log1p(-jnp.exp2(-5.0 - jnp.arange(H, dtype=jnp.float32)))
    idx = jnp.arange(CHUNK, dtype=jnp.float32)
    intra_dec = jnp.exp(jnp.abs(idx[:, None] - idx[None, :])[None] * log_g[:, None, None])
    q_dec = jnp.exp((idx + 1.0)[:, None] * log_g[None, :])
    k_dec = jnp.exp((CHUNK - 1.0 - idx)[:, None] * log_g[None, :])
    chunk_dec = jnp.exp(CHUNK * log_g)

    qc = q.reshape(B, nc, CHUNK, H, dk)
    kc = k.reshape(B, nc, CHUNK, H, dk)
    vc = v.reshape(B, nc, CHUNK, H, dv)

    scores = jnp.einsum('bnchd,bnshd->bnhcs', qc, kc) * intra_dec
    o_intra = jnp.einsum('bnhcs,bnshv->bnchv', scores, vc)

    def step(state, inp):
        qi, ki, vi = inp
        o = jnp.einsum('bchk,bhkv->bchv', qi * q_dec[None, :, :, None], state)
        state = chunk_dec[None, :, None, None] * state + jnp.einsum(
            'bchk,bchv->bhkv', ki * k_dec[None, :, :, None], vi)
        return state, o

    xs = (jnp.moveaxis(qc, 1, 0), jnp.moveaxis(kc, 1, 0), jnp.moveaxis(vc, 1, 0))
    s0 = jnp.zeros((B, H, dk, dv), jnp.float32)
    _, o_cross = lax.scan(step, s0, xs)
    o = o_intra + jnp.moveaxis(o_cross, 0, 1)
    return o.reshape(B, S, H, dv)


def _head_groupnorm(o, w):
    mu = jnp.mean(o, axis=-1, keepdims=True)
    var = jnp.mean(jnp.square(o - mu), axis=-1, keepdims=True)
    y = (o - mu) * lax.rsqrt(var + EPS)
    B, S = o.shape[0], o.shape[1]
    return y.reshape(B, S, -1) * w.astype(jnp.float32)


def _causal_conv(z, w):
    S = z.shape[1]
    zp = jnp.pad(z, ((0, 0), (CONV_K - 1, 0), (0, 0)))
    out = zp[:, 0:S] * w[0]
    for j in range(1, CONV_K):
        out = out + zp[:, j:j + S] * w[j]
    return out


def setup_inputs(seed: int = 0) -> dict:
    key = jax.random.key(seed)
    ks = jax.random.split(key, 18)
    f32 = jnp.float32
    nrm = lambda k, shape, s: jax.random.normal(k, shape, f32) * s
    x = jax.random.normal(ks[0], (BATCH, SEQ, D_MODEL), f32)
    c = jax.random.normal(ks[1], (BATCH, D_MODEL), f32)
    positions = jnp.broadcast_to(jnp.arange(SEQ, dtype=jnp.int32)[None, :], (BATCH, SEQ))
    return {
        "x": x,
        "c": c,
        "positions": positions,
        "w_ada": nrm(ks[2], (DEPTH, D_MODEL, N_MOD * D_MODEL), 0.5 * D_MODEL ** -0.5),
        "b_ada": nrm(ks[3], (DEPTH, N_MOD * D_MODEL), 0.01),
        "norm1_w": 1.0 + nrm(ks[4], (DEPTH, D_MODEL), 0.02),
        "w_in": nrm(ks[5], (DEPTH, D_MODEL, IN_WIDTH), D_MODEL ** -0.5),
        "ret_gn_w": 1.0 + nrm(ks[6], (DEPTH, RET_V), 0.02),
        "conv_w": nrm(ks[7], (DEPTH, CONV_K, CONV_WIDTH), CONV_K ** -0.5),
        "w_ret_out": nrm(ks[8], (DEPTH, RET_V, D_MODEL), RET_V ** -0.5),
        "w_conv_out": nrm(ks[9], (DEPTH, CONV_WIDTH, D_MODEL), CONV_WIDTH ** -0.5),
        "w_o": nrm(ks[10], (DEPTH, D_MODEL, D_MODEL), D_MODEL ** -0.5),
        "norm2_w": 1.0 + nrm(ks[11], (DEPTH, D_MODEL), 0.02),
        "w_ff1": nrm(ks[12], (DEPTH, D_MODEL, D_FF), D_MODEL ** -0.5),
        "w_ff2": nrm(ks[13], (DEPTH, D_FF, D_MODEL), D_FF ** -0.5),
        "final_norm_w": 1.0 + nrm(ks[14], (D_MODEL,), 0.02),
    }


def reference(x, c, positions, w_ada, b_ada, norm1_w, w_in, ret_gn_w, conv_w,
              w_ret_out, w_conv_out, w_o, norm2_w, w_ff1, w_ff2, final_norm_w):
    B, S, _ = x.shape
    split_points = [int(i) for i in np.cumsum(SPLITS)[:-1]]
    h = x
    for l in range(DEPTH):
        mod = jnp.einsum('bd,de->be', jax.nn.silu(c), w_ada[l]) + b_ada[l]
        sh1, sc1, g1, sh2, sc2, g2 = jnp.split(mod[:, None, :], N_MOD, axis=-1)

        u = _rmsnorm(h, norm1_w[l]) * (1.0 + sc1) + sh1
        proj = jnp.einsum('bsd,de->bse', u, w_in[l])
        q, k, v, og, cb, cc, cx, ga, gb = jnp.split(proj, split_points, axis=-1)

        qh = _rotary(q.astype(jnp.float32).reshape(B, S, RET_HEADS, RET_QK_DIM), positions)
        kh = _rotary(k.astype(jnp.float32).reshape(B, S, RET_HEADS, RET_QK_DIM), positions) * (RET_QK_DIM ** -0.5)
        vh = v.astype(jnp.float32).reshape(B, S, RET_HEADS, RET_V_DIM)
        ret = _head_groupnorm(_retention(qh, kh, vh), ret_gn_w[l]).astype(x.dtype)
        y_ret = jnp.einsum('bse,ed->bsd', jax.nn.silu(og) * ret, w_ret_out[l])

        y_conv = jnp.einsum('bse,ed->bsd', cb * _causal_conv(cc * cx, conv_w[l]), w_conv_out[l])

        merged = jax.nn.sigmoid(ga) * y_ret + jax.nn.sigmoid(gb) * y_conv
        h = h + g1 * jnp.einsum('bsd,de->bse', merged, w_o[l])

        u2 = _rmsnorm(h, norm2_w[l]) * (1.0 + sc2) + sh2
        hid = jnp.square(jax.nn.relu(jnp.einsum('bsd,df->bsf', u2, w_ff1[l])))
        h = h + g2 * jnp.einsum('bsf,fd->bsd', hid, w_ff2[l])
    return _rmsnorm(h, final_norm_w)
```

```python
import numpy as np
import ml_dtypes
import concourse.bass as bass
import concourse.mybir as mybir
from concourse.bass_utils import run_bass_kernel_spmd
from contextlib import ExitStack

F32 = mybir.dt.float32
BF16 = mybir.dt.bfloat16
I32 = mybir.dt.int32
AF = mybir.ActivationFunctionType
ALU = mybir.AluOpType

ENGS = ("pe", "act", "dve", "pool", "sp")
EPS = 1e-6
NSLOT = 5
TWO_PI = float(2.0 * np.pi)
PI = float(np.pi)


class Op:
    __slots__ = ("eng", "fn", "deps", "signal", "count", "dma_key", "dma_count", "idx")

    def __init__(self, eng, fn):
        self.eng = eng
        self.fn = fn
        self.deps = set()
        self.signal = False
        self.count = None
        self.dma_key = None
        self.dma_count = None


class Sched:
    def __init__(self):
        self.ops = {e: [] for e in ENGS}
        self.last_writer = {}
        self.readers = {}
        self.last_dma = {}
        self.dma_counts = {}

    def op(self, eng, fn, reads=(), writes=(), dma_key=None):
        o = Op(eng, fn)
        deps = set()
        for r in reads:
            w = self.last_writer.get(r)
            if w is not None:
                deps.add(w)
        for w_ in writes:
            lw = self.last_writer.get(w_)
            if lw is not None:
                deps.add(lw)
            for rd in self.readers.get(w_, ()):
                deps.add(rd)
        if dma_key is not None:
            o.dma_key = dma_key
            prev = self.last_dma.get(dma_key)
            if prev is not None:
                deps.add(prev)
            self.last_dma[dma_key] = o
            c = self.dma_counts.get(dma_key, 0) + 16
            self.dma_counts[dma_key] = c
            o.dma_count = c
        deps.discard(o)
        o.deps = deps
        for r in reads:
            self.readers.setdefault(r, []).append(o)
        for w_ in writes:
            self.last_writer[w_] = o
            self.readers[w_] = []
        self.ops[eng].append(o)
        return o

    def emit(self, nc, final_ops):
        def needs_sem(d, o):
            return not (d.eng == "pe" and o.eng == "pe" and d.dma_key is None and o.dma_key is None)

        for e in ENGS:
            for o in self.ops[e]:
                for d in o.deps:
                    if d.dma_key is None and needs_sem(d, o):
                        d.signal = True
        for e in ENGS:
            c = 0
            for o in self.ops[e]:
                if o.signal and o.dma_key is None:
                    c += 1
                    o.count = c
        dma_keys = list(self.dma_counts.keys())
        with ExitStack() as es:
            esem = {e: es.enter_context(nc.semaphore("s_" + e)) for e in ENGS}
            dsem = {k: es.enter_context(nc.semaphore("d_%d" % i)) for i, k in enumerate(dma_keys)}
            block = es.enter_context(nc.Block())

            def run_engine(e, eng):
                seen = {}
                for o in self.ops[e]:
                    need = {}
                    for d in o.deps:
                        if d.dma_key is not None:
                            key, val = ("d", d.dma_key), d.dma_count
                        else:
                            if not needs_sem(d, o):
                                continue
                            key, val = ("e", d.eng), d.count
                        if val > need.get(key, 0):
                            need[key] = val
                    for key, val in need.items():
                        if seen.get(key, 0) >= val:
                            continue
                        seen[key] = val
                        eng.wait_ge(dsem[key[1]] if key[0] == "d" else esem[key[1]], val)
                    ins = o.fn(eng)
                    if o.dma_key is not None:
                        ins.then_inc(dsem[o.dma_key], 16)
                    elif o.signal:
                        ins.then_inc(esem[e], 1)
                if e == "sp":
                    for o in final_ops:
                        eng.wait_ge(dsem[o.dma_key], o.dma_count)

            @block.tensor
            def _(eng):
                run_engine("pe", eng)

            @block.scalar
            def _(eng):
                run_engine("act", eng)

            @block.vector
            def _(eng):
                run_engine("dve", eng)

            @block.gpsimd
            def _(eng):
                run_engine("pool", eng)

            @block.sync
            def _(eng):
                run_engine("sp", eng)


def build_program(dumps=(), stop_after=None):
    nc = bass.Bass("TRN2", target_bir_lowering=False)
    S = Sched()

    def din(name, shape, dt=F32):
        return nc.dram_tensor(name, list(shape), dt, kind="ExternalInput").ap()

    x_main = din("x_main", [2048, 1024])
    x_pre = din("x_pre", [2048, 1024])
    cT_d = din("cT", [128, 8])
    posm_d = din("pos_m", [128, 16], I32)
    posp_d = din("pos_p", [128, 16], I32)
    w_ada = din("w_ada", [1024, 6144])
    b_ada_rep = din("b_ada_rep", [128, 6144])
    n1w_d = din("n1w_rep", [128, 1024])
    n2w_d = din("n2w_rep", [128, 1024])
    fnw_d = din("fnw_rep", [128, 1024])
    gnw_d = din("gnw_rep", [128, 1024])
    w_in = din("w_in", [1024, 8192])
    w_ro = din("w_ro", [1024, 1024])
    w_co = din("w_co", [1024, 1024])
    w_o = din("w_o", [1024, 1024])
    w_ff1 = din("w_ff1", [1024, 4096])
    w_ff2 = din("w_ff2", [4096, 1024])
    convw_d = din("convw", [128, 24])
    invf_d = din("invf", [128, 64])
    mask_d = din("mask", [128, 512])
    qdec_d = din("qdec_rep", [128, 4])
    kdec_d = din("kdec_rep", [128, 4])
    predec_d = din("predec", [128, 64])
    flag_d = din("flagcol", [128, 1])
    ident_d = din("ident", [128, 128], BF16)
    y_out = nc.dram_tensor("y", [2048, 1024], F32, kind="ExternalOutput").ap()
    h1s = nc.dram_tensor("h1s", [2048, 1024], F32, kind="Internal").ap()
    dump_ops = []

    with ExitStack() as es:
        def sb(name, shape, dt=F32):
            return es.enter_context(nc.sbuf_tensor(name, list(shape), dt))

        def ps(name, shape, dt=F32):
            return es.enter_context(nc.psum_tensor(name, list(shape), dt))

        X = sb("X", [128, 8, 2048], BF16)
        Y = sb("Y", [128, 8, 2048], BF16)
        Z = sb("Z", [128, 8, 2048], BF16)
        ring_t = [sb("ring%d" % i, [128, 8, 512], BF16) for i in range(NSLOT)]
        AR = sb("AR", [128, 4096], F32)
        cosT = AR[:, 0:1024].rearrange("p (t i) -> p t i", i=64)
        sinT = AR[:, 1024:2048].rearrange("p (t i) -> p t i", i=64)
        S_t = AR[:, 2048:3072]
        B_t = AR[:, 3072:4096]
        G_t = sb("G_t", [128, 1024], F32)
        scr = [sb("scr%d" % i, [128, 1024], F32) for i in range(7)]
        xsT = sb("xsT", [128, 3072], BF16)
        xs_b = [xsT[:, 0:1024], xsT[:, 1024:2048], xsT[:, 2048:3072]]
        pa = [ps("pa%d" % i, [128, 512], F32) for i in range(6)]
        pt = [ps("pt%d" % i, [128, 1024], BF16) for i in range(2)]
        ident = sb("ident_s", [128, 128], BF16)
        ones_bf = sb("ones_bf", [128, 128], BF16)
        cT = sb("cT_s", [128, 8], F32)
        csil = sb("csil", [128, 8], F32)
        crep = sb("crep", [128, 8, 128], BF16)
        posi = sb("posi", [128, 16], I32)
        posf = sb("posf", [128, 16], F32)
        invf = sb("invf_s", [128, 64], F32)
        mask = sb("mask_s", [128, 512], F32)
        qdec = sb("qdec_s", [128, 4], F32)
        kdec = sb("kdec_s", [128, 4], F32)
        predec = sb("predec_s", [128, 64], F32)
        flagc = sb("flag_s", [128, 1], F32)
        convw = sb("convw_s", [128, 24], F32)
        mhalf = sb("mhalf", [128, 4], F32)
        ss = sb("ss", [128, 96], F32)
        var = sb("var", [128, 96], F32)
        rstd = sb("rstd", [128, 96], F32)
        uTp = [Z[:, i, 0:1024].rearrange("p (k m) -> p k m", k=8) for i in range(2)]
        uTpK = [[("Z", i, 0), ("Z", i, 1)] for i in range(2)]
        uTh = sb("uTh", [128, 8, 2], BF16)
        zhalo = sb("zhalo", [128, 16], F32)
        state = sb("state", [128, 1024], F32)
        state_bf = sb("state_bf", [128, 1024], BF16)
        vt_b = [sb("vt%d" % i, [128, 1024], BF16) for i in range(2)]
        ktp = [Z[:, 2 + i, 0:512] for i in range(2)]
        bn6 = sb("bn6", [128, 2, 4, 6], F32)
        mv = sb("mv", [128, 2, 4, 2], F32)
        gve = sb("gve", [128, 2, 4], F32)
        grs = sb("grs", [128, 2, 4], F32)
        gnm = sb("gnm", [128, 2, 4], F32)

        R8 = range(8)

        def SK(i):
            return [("scr", i, 0), ("scr", i, 1)]

        def SH(i, h):
            return ("scr", i, h)

        def XK(name, ks, tbs):
            return [(name, k, tb) for k in ks for tb in tbs]

        def dma(q, out, in_, reads, writes, key):
            return S.op(q, lambda e: e.dma_start(out=out, in_=in_), reads=reads, writes=writes, dma_key=key)

        def dump(name, ap, shape, dt, reads):
            if name not in dumps:
                return
            d = nc.dram_tensor("dbg_" + name, list(shape), dt, kind="ExternalOutput").ap()
            dump_ops.append(dma("sp", d, ap, reads, [], "dbg_" + name))

        def wchunk(w, r0, c0):
            return w[r0:r0 + 1024, c0:c0 + 512].rearrange("(kc p) n -> p kc n", p=128)

        chunks = []
        for v in (1, 0):
            chunks += [wchunk(w_ada, 0, v * 1024 + n * 512) for n in range(2)]
        chunks += [wchunk(w_in, 0, 512), wchunk(w_in, 0, 1024), wchunk(w_in, 0, 1536)]
        for jg in range(2):
            chunks += [wchunk(w_in, 0, 4096 + jg * 512), wchunk(w_in, 0, 5120 + jg * 512),
                       wchunk(w_in, 0, 3072 + jg * 512)]
        chunks += [wchunk(w_ada, 0, 2 * 1024 + n * 512) for n in range(2)]
        for jg in range(2):
            chunks += [wchunk(w_co, 0, jg * 512), wchunk(w_in, 0, 7168 + jg * 512)]
        chunks += [wchunk(w_in, 0, 0), wchunk(w_in, 0, 512), wchunk(w_in, 0, 1024), wchunk(w_in, 0, 1536)]
        chunks += [wchunk(w_in, 0, 2048), wchunk(w_in, 0, 2560)]
        for v in (4, 3):
            chunks += [wchunk(w_ada, 0, v * 1024 + n * 512) for n in range(2)]
        for jg in range(2):
            chunks += [wchunk(w_ro, 0, jg * 512), wchunk(w_in, 0, 6144 + jg * 512)]
        chunks += [wchunk(w_o, 0, 0), wchunk(w_o, 0, 512)]
        chunks += [wchunk(w_ada, 0, 5 * 1024 + n * 512) for n in range(2)]
        for mb in range(2):
            chunks += [wchunk(w_ff1, 0, fc * 512) for fc in range(8)]
            for n in range(2):
                chunks += [wchunk(w_ff2, fg * 1024, n * 512) for fg in range(4)]

        rstate = {"load": 0, "use": 0, "rel": 0}

        def ring_fill():
            while rstate["load"] < len(chunks) and rstate["load"] - rstate["rel"] < NSLOT:
                i = rstate["load"]
                s = i % NSLOT
                dma("pool", ring_t[s][:, :, :], chunks[i], [], [("ring", s)], ("ring", s))
                rstate["load"] += 1

        def ring_use():
            i = rstate["use"]
            assert i < rstate["load"], "ring underflow (too many resident chunks)"
            rstate["use"] += 1
            return ring_t[i % NSLOT], ("ring", i % NSLOT)

        def ring_rel(n=1):
            rstate["rel"] += n
            assert rstate["rel"] <= rstate["use"]
            ring_fill()

        bank_ctr = [0]

        def nbank():
            b = bank_ctr[0] % 6
            bank_ctr[0] += 1
            return b

        def mm(out, okey, lhsT, lkeys, rhs, rkeys, start, stop):
            S.op("pe", lambda e: e.matmul(out, lhsT, rhs, start=start, stop=stop),
                 reads=list(lkeys) + list(rkeys), writes=[okey])

        def cload(t, d, key):
            dma("sp", t, d, [], [key], key)

        cload(cT[:, :], cT_d, "cT")
        cload(ident[:, :], ident_d, "ident")
        ring_fill()
        cload(posi[:, :], posp_d, "posi")
        cload(invf[:, :], invf_d, "invf")
        cload(predec[:, :], predec_d, "predec")
        cload(kdec[:, :], kdec_d, "kdec")
        cload(qdec[:, :], qdec_d, "qdec")
        cload(mask[:, :], mask_d, "mask")
        cload(flagc[:, :], flag_d, "flagc")
        cload(convw[:, :], convw_d, "convw")
        S.op("dve", lambda e: e.memset(ones_bf[:, :], 1.0), writes=["ones"])
        S.op("dve", lambda e: e.memset(mhalf[:, :], -0.5), writes=["mhalf"])
        S.op("act", lambda e: e.activation(out=csil[:, :], in_=cT[:, :], func=AF.Silu), reads=["cT"], writes=["csil"])
        for k in R8:
            S.op("dve", lambda e, k=k: e.tensor_scalar(out=crep[:, k, :], in0=ones_bf[:, :], scalar1=csil[:, k:k + 1],
                                                      scalar2=None, op0=ALU.mult),
                 reads=["ones", "csil"], writes=[("crep", k)])

        def build_tables(defer=False):
            S.op("dve", lambda e: e.tensor_copy(out=posf[:, :], in_=posi[:, :]), reads=["posi"], writes=["posf"])
            ang = scr[5][:, :].rearrange("p (t i) -> p t i", i=64)
            tmpi = scr[6][:, :].bitcast(I32)
            angf = scr[5][:, :]
            fx = scr[6][:, :]
            for t in range(16):
                S.op("dve", lambda e, t=t: e.tensor_scalar(out=ang[:, t, :], in0=invf[:, :], scalar1=posf[:, t:t + 1],
                                                          scalar2=None, op0=ALU.mult),
                     reads=["posf", "invf"], writes=SK(5))
            acts = []
            for which, shift, dst, bi in (("sin", 0.0, AR[:, 1024:2048], 4), ("cos", PI / 2, AR[:, 0:1024], 3)):
                tmpf = scr[bi][:, :]
                BK = SK(bi)
                S.op("dve", lambda e, shift=shift, tmpf=tmpf: e.tensor_scalar(out=tmpf, in0=angf, scalar1=shift, scalar2=1.0 / TWO_PI,
                                                                               op0=ALU.add, op1=ALU.mult),
                     reads=SK(5), writes=BK)
                S.op("dve", lambda e, tmpf=tmpf: e.tensor_copy(out=tmpi, in_=tmpf), reads=BK, writes=SK(6))
                S.op("dve", lambda e, tmpf=tmpf: e.tensor_copy(out=tmpf, in_=tmpi), reads=SK(6), writes=BK)
                S.op("dve", lambda e, tmpf=tmpf: e.scalar_tensor_tensor(out=tmpf, in0=tmpf, scalar=-TWO_PI, in1=angf,
                                                                        op0=ALU.mult, op1=ALU.add),
                     reads=SK(5) + BK, writes=BK)
                S.op("dve", lambda e, shift=shift, tmpf=tmpf: e.tensor_scalar(out=fx, in0=tmpf, scalar1=PI - shift, scalar2=-TWO_PI,
                                                                               op0=ALU.is_gt, op1=ALU.mult),
                     reads=BK, writes=SK(6))
                S.op("dve", lambda e, shift=shift, tmpf=tmpf: e.scalar_tensor_tensor(out=tmpf, in0=tmpf, scalar=shift, in1=fx,
                                                                                      op0=ALU.add, op1=ALU.add),
                     reads=BK + SK(6), writes=BK)
                S.op("dve", lambda e, tmpf=tmpf: e.tensor_scalar(out=tmpf, in0=tmpf, scalar1=PI, scalar2=-PI, op0=ALU.min, op1=ALU.max),
                     reads=BK, writes=BK)
                key = ("AR", 1 if which == "sin" else 0)
                acts.append((dst, tmpf, BK, key))

            def do_acts():
                for dst, tmpf, BK, key in acts:
                    S.op("act", lambda e, dst=dst, tmpf=tmpf: e.activation(out=dst, in_=tmpf, func=AF.Sin), reads=BK, writes=[key])
            if defer:
                return do_acts
            do_acts()
            return None

        tables_act = build_tables(defer=True)

        btmp = scr[6][:, 512:1024]

        def modvec(v, out_ap, outkeys):
            for n in range(2):
                slot, skey = ring_use()
                dma("sp", btmp, b_ada_rep[:, v * 1024 + n * 512: v * 1024 + (n + 1) * 512], [], [SH(6, 1)], "bt")
                b = nbank()
                for k in R8:
                    mm(pa[b][:, :], ("pa", b), crep[:, k, :], [("crep", k)], slot[:, k, :], [skey], k == 0, k == 7)
                ring_rel()
                S.op("dve", lambda e, b=b, n=n: e.tensor_tensor(out=out_ap[:, n * 512:(n + 1) * 512], in0=pa[b][:, :],
                                                               in1=btmp, op=ALU.add),
                     reads=[("pa", b), SH(6, 1)], writes=outkeys)

        def make_SB(v_sc, v_sh, nw_d):
            modvec(v_sc, scr[3], SK(3))
            dma("sp", scr[4][:, :], nw_d, [], SK(4), "nw")
            S.op("dve", lambda e: e.scalar_tensor_tensor(out=S_t, in0=scr[3][:, :], scalar=1.0, in1=scr[4][:, :],
                                                         op0=ALU.add, op1=ALU.mult),
                 reads=SK(3) + SK(4), writes=[("AR", 2)])
            modvec(v_sh, B_t, [("AR", 3)])

        dump("S1", S_t, [128, 1024], F32, [("AR", 2)])
        dump("B1", B_t, [128, 1024], F32, [("AR", 3)])
        dump("cos", AR[:, 0:1024], [128, 1024], F32, [("AR", 0)])
        dump("sin", AR[:, 1024:2048], [128, 1024], F32, [("AR", 1)])

        ctr = {"xt": 0, "tmp": 0, "xs": 0, "col": 0, "pt": 0}

        def norm_stats(xt_ap, xkeys):
            col = ctr["col"]
            ctr["col"] += 1
            ti = 3 + ctr["tmp"] % 2
            S.op("act", lambda e: e.activation(out=scr[ti][:, :], in_=xt_ap, func=AF.Square, accum_out=ss[:, col:col + 1]),
                 reads=xkeys, writes=SK(ti) + [("ss", col)])
            S.op("dve", lambda e: e.tensor_scalar(out=var[:, col:col + 1], in0=ss[:, col:col + 1], scalar1=1.0 / 1024,
                                                  scalar2=EPS, op0=ALU.mult, op1=ALU.add),
                 reads=[("ss", col)], writes=[("var", col)])
            S.op("pool", lambda e: e.tensor_tensor(out=rstd[:, col:col + 1], in0=var[:, col:col + 1], in1=mhalf[:, 0:1], op=ALU.pow),
                 reads=[("var", col), "mhalf"], writes=[("rstd", col)])
            return rstd[:, col:col + 1], ("rstd", col)

        def norm_mod(xt_ap, xkeys, add_eng="pool"):
            r_ap, rkey = norm_stats(xt_ap, xkeys)
            ti = 3 + ctr["tmp"] % 2
            ctr["tmp"] += 1
            xi = ctr["xs"] % 3
            ctr["xs"] += 1
            S.op("dve", lambda e: e.scalar_tensor_tensor(out=scr[ti][:, :], in0=xt_ap, scalar=r_ap, in1=S_t,
                                                         op0=ALU.mult, op1=ALU.mult),
                 reads=list(xkeys) + [rkey, ("AR", 2)], writes=SK(ti))
            S.op(add_eng, lambda e: e.tensor_tensor(out=xs_b[xi], in0=scr[ti][:, :], in1=B_t, op=ALU.add),
                 reads=SK(ti) + [("AR", 3)], writes=[("xs", xi)])
            return xi

        def nA(xt_ap, xkeys):
            col = ctr["col"]
            ctr["col"] += 1
            S.op("act", lambda e: e.activation(out=Y[:, 7, 0:1024], in_=xt_ap, func=AF.Square, accum_out=ss[:, col:col + 1]),
                 reads=xkeys, writes=[("Y", 7, 0), ("Y", 7, 1), ("ss", col)])
            return dict(col=col, x=xt_ap, xk=list(xkeys))

        def nB(h):
            col = h["col"]
            S.op("dve", lambda e: e.tensor_scalar(out=var[:, col:col + 1], in0=ss[:, col:col + 1], scalar1=1.0 / 1024,
                                                  scalar2=EPS, op0=ALU.mult, op1=ALU.add),
                 reads=[("ss", col)], writes=[("var", col)])
            S.op("pool", lambda e: e.tensor_tensor(out=rstd[:, col:col + 1], in0=var[:, col:col + 1], in1=mhalf[:, 0:1], op=ALU.pow),
                 reads=[("var", col), "mhalf"], writes=[("rstd", col)])

        def nC2(h, tmp_ap, tmp_keys, xs_ap, xs_keys, add_eng):
            col = h["col"]
            S.op("dve", lambda e: e.scalar_tensor_tensor(out=tmp_ap, in0=h["x"], scalar=rstd[:, col:col + 1], in1=S_t,
                                                         op0=ALU.mult, op1=ALU.mult),
                 reads=h["xk"] + [("rstd", col), ("AR", 2)], writes=tmp_keys)
            S.op(add_eng, lambda e: e.tensor_tensor(out=xs_ap, in0=tmp_ap, in1=B_t, op=ALU.add),
                 reads=tmp_keys + [("AR", 3)], writes=xs_keys)

        def nC(h, add_eng="dve"):
            col = h["col"]
            ti = 3 + ctr["tmp"] % 2
            ctr["tmp"] += 1
            xi = ctr["xs"] % 3
            ctr["xs"] += 1
            S.op("dve", lambda e: e.scalar_tensor_tensor(out=scr[ti][:, :], in0=h["x"], scalar=rstd[:, col:col + 1], in1=S_t,
                                                         op0=ALU.mult, op1=ALU.mult),
                 reads=h["xk"] + [("rstd", col), ("AR", 2)], writes=SK(ti))
            S.op(add_eng, lambda e: e.tensor_tensor(out=xs_b[xi], in0=scr[ti][:, :], in1=B_t, op=ALU.add),
                 reads=SK(ti) + [("AR", 3)], writes=[("xs", xi)])
            return xi

        def transpose8(src_bf, srckey):
            pi = ctr["pt"] % 2
            ctr["pt"] += 1
            for k in R8:
                S.op("pe", lambda e, k=k: e.transpose(pt[pi][:, k * 128:(k + 1) * 128], src_bf[:, k * 128:(k + 1) * 128], ident[:, :]),
                     reads=[srckey, "ident"], writes=[("pt", pi)])
            return pi

        def load_x(src, t, reads=()):
            xi = ctr["xt"] % 3
            ctr["xt"] += 1
            dma("sp", scr[xi][:, :], src[t * 128:(t + 1) * 128, :], list(reads), SK(xi), ("xt", xi))
            return xi

        def rotary(b, t, dec_fn, which=0, defer=False):
            xv = pa[b][:, :].rearrange("p (h two i) -> p h two i", h=4, two=2)
            if which == 0:
                A, Bq, ka, kb = scr[5][:, 0:512], scr[5][:, 512:1024], SH(5, 0), SH(5, 1)
            else:
                A, Bq, ka, kb = scr[4][:, 512:1024], scr[6][:, 0:512], SH(4, 1), SH(6, 0)
            A4 = A.rearrange("p (h two i) -> p h two i", h=4, two=2)
            B4 = Bq.rearrange("p (h two i) -> p h two i", h=4, two=2)
            cos4 = cosT[:, t, :].unsqueeze(1).unsqueeze(1).to_broadcast([128, 4, 2, 64])
            sin3 = sinT[:, t, :].unsqueeze(1).to_broadcast([128, 4, 64])
            S.op("dve", lambda e: e.tensor_tensor(out=A4, in0=xv, in1=cos4, op=ALU.mult),
                 reads=[("pa", b), ("AR", 0)], writes=[ka])
            S.op("dve", lambda e: e.scalar_tensor_tensor(out=B4[:, :, 0, :], in0=xv[:, :, 1, :], scalar=-1.0, in1=sin3,
                                                         op0=ALU.mult, op1=ALU.mult),
                 reads=[("pa", b), ("AR", 1)], writes=[kb])
            S.op("dve", lambda e: e.tensor_tensor(out=B4[:, :, 1, :], in0=xv[:, :, 0, :], in1=sin3, op=ALU.mult),
                 reads=[("pa", b), ("AR", 1)], writes=[kb])
            def padd():
                S.op("pool", lambda e: e.tensor_tensor(out=A, in0=A, in1=Bq, op=ALU.add),
                     reads=[ka, kb], writes=[ka])

            def tail():
                padd()
                dec_fn(A, ka)
            if defer == "dec":
                padd()
                return lambda: dec_fn(A, ka)
            if defer:
                return tail
            tail()
            return None

        st_banks = (4, 5)
        pslots = {}
        pre_dec = {}

        def pre_proj(t):
            ui = t % 2
            bk, bv0, bv1 = (0, 1, 2) if t % 2 == 0 else (3, 1, 2)
            for (slot, skey, b) in ((pslots["k"] + (bk,)), (pslots["v0"] + (bv0,)), (pslots["v1"] + (bv1,))):
                for k in R8:
                    mm(pa[b][:, :], ("pa", b), uTp[ui][:, k, :], uTpK[ui], slot[:, k, :], [skey], k == 0, k == 7)

            def dec_fn(Rr, rkey):
                pd3 = predec[:, t * 4:(t + 1) * 4].unsqueeze(2).to_broadcast([128, 4, 128])
                S.op("pool", lambda e: e.tensor_tensor(out=ktp[ui].rearrange("p (h d) -> p h d", h=4),
                                                       in0=Rr.rearrange("p (h d) -> p h d", h=4), in1=pd3, op=ALU.mult),
                     reads=[rkey, "predec"], writes=[("Z", 2 + ui, 0)])
            pre_dec[t] = rotary(bk, t, dec_fn, 0, defer="dec")
            S.op("act", lambda e: e.copy(out=vt_b[ui][:, 0:512], in_=pa[bv0][:, :]), reads=[("pa", bv0)], writes=[("vt", ui)])
            S.op("act", lambda e: e.copy(out=vt_b[ui][:, 512:1024], in_=pa[bv1][:, :]), reads=[("pa", bv1)], writes=[("vt", ui)])

        S.op("dve", lambda e: e.memset(state[:, :], 0.0), writes=[("state", h) for h in range(4)])

        def pre_state(t):
            ui = t % 2
            for h in range(4):
                b = st_banks[h // 2]
                mm(pa[b][:, (h % 2) * 256:(h % 2 + 1) * 256], ("pa", b), ktp[ui][:, h * 128:(h + 1) * 128], [("Z", 2 + ui, 0)],
                   vt_b[ui][:, h * 256:(h + 1) * 256], [("vt", ui)], True, True)
            for h2 in range(2):
                b = st_banks[h2]
                S.op("dve", lambda e, b=b, h2=h2: e.tensor_tensor(out=state[:, h2 * 512:(h2 + 1) * 512], in0=state[:, h2 * 512:(h2 + 1) * 512],
                                                                 in1=pa[b][:, :], op=ALU.add),
                     reads=[("pa", b), ("state", 2 * h2), ("state", 2 * h2 + 1)], writes=[("state", 2 * h2), ("state", 2 * h2 + 1)])

        pre_h = {}
        pre_xs = {}

        def pre_T(t):
            xsi = pre_xs[t]
            pi = transpose8(xs_b[xsi], ("xs", xsi))
            ui = t % 2
            S.op("act", lambda e: e.copy(out=uTp[ui], in_=pt[pi][:, :].rearrange("p (k m) -> p k m", k=8)),
                 reads=[("pt", pi)], writes=uTpK[ui])
            if t == 15:
                S.op("dve", lambda e: e.tensor_copy(out=uTh[:, :, :], in_=uTp[ui][:, :, 126:128]), reads=uTpK[ui], writes=["uTh"])

        YK = lambda k: [("Y", k, tb) for tb in range(4)]
        mx = [Y[:, i, :].bitcast(F32) for i in range(3)]
        mtmp = [Y[:, 3 + i, :].bitcast(F32) for i in range(2)]
        mxs = [Y[:, 5, 0:1024], Y[:, 5, 1024:2048], Y[:, 6, 0:1024]]
        mxsK = [[("Y", 5, 0), ("Y", 5, 1)], [("Y", 5, 2), ("Y", 5, 3)], [("Y", 6, 0), ("Y", 6, 1)]]
        m_h = {}

        def main_T(t):
            x3 = t % 3
            pi = transpose8(mxs[x3], mxsK[x3][0])
            S.op("act", lambda e: e.copy(out=X[:, :, t * 128:(t + 1) * 128], in_=pt[pi][:, :].rearrange("p (k m) -> p k m", k=8)),
                 reads=[("pt", pi)] + mxsK[x3], writes=XK("X", R8, [t // 4]))

        for it in range(-4, 17):
            if 0 <= it + 4 < 16:
                t4 = it + 4
                xi = load_x(x_pre, t4)
                pre_h[t4] = nA(scr[xi][:, :], SK(xi))
                dma("sp", mx[t4 % 3], x_main[t4 * 128:(t4 + 1) * 128, :], [], YK(t4 % 3), ("mx", t4 % 3))
                m_h[t4] = nA(mx[t4 % 3], YK(t4 % 3))
            if 0 <= it + 3 < 16:
                nB(pre_h[it + 3])
                nB(m_h[it + 3])
            if it == -2:
                tables_act()
                make_SB(1, 0, n1w_d)
                pslots["k"] = ring_use()
                pslots["v0"] = ring_use()
                pslots["v1"] = ring_use()
            if 0 <= it + 2 < 16:
                t2 = it + 2
                pre_xs[t2] = nC(pre_h[t2])
                nC2(m_h[t2], mtmp[t2 % 2], YK(3 + t2 % 2), mxs[t2 % 3], mxsK[t2 % 3], "pool")
            if 0 <= it - 1 < 16:
                pre_dec.pop(it - 1)()
            if 0 <= it + 1 < 16:
                pre_T(it + 1)
                main_T(it + 1)
            if 0 <= it < 16:
                pre_proj(it)
            if 0 <= it - 1 < 16:
                pre_state(it - 1)
        ring_rel(3)
        S.op("act", lambda e: e.copy(out=state_bf[:, :], in_=state[:, :]),
             reads=[("state", h) for h in range(4)], writes=[("state_bf", h) for h in range(4)])
        dump("state", state[:, :], [128, 1024], F32, [("state", h) for h in range(4)])
        dump("uT", X[:, :, :], [128, 8, 2048], BF16, XK("X", R8, range(4)))
        cload(posi[:, :], posm_d, "posi")
        build_tables()
        if stop_after == "U":
            S.emit(nc, dump_ops)
            return nc

        zb = [scr[3][:, 0:514], scr[4][:, 0:514]]
        for jg in range(2):
            s_cc, k_cc = ring_use()
            s_cx, k_cx = ring_use()
            s_cb, k_cb = ring_use()
            bh = nbank()
            for jj in range(4):
                for (slot, skey, off) in ((s_cc, k_cc, 0), (s_cx, k_cx, 8)):
                    for k in R8:
                        mm(pa[bh][:, off + jj * 2: off + jj * 2 + 2], ("pa", bh), slot[:, k, jj * 128:(jj + 1) * 128], [skey],
                           uTh[:, k, :], ["uTh"], k == 0, k == 7)
            S.op("act", lambda e, bh=bh: e.copy(out=zhalo[:, 0:8], in_=pa[bh][:, 0:8]), reads=[("pa", bh)], writes=["zh_a"])
            S.op("dve", lambda e, bh=bh: e.scalar_tensor_tensor(out=zhalo[:, 8:16], in0=pa[bh][:, 8:16], scalar=flagc[:, 0:1],
                                                               in1=zhalo[:, 0:8], op0=ALU.mult, op1=ALU.mult),
                 reads=[("pa", bh), "zh_a", "flagc"], writes=["zh_b"])
            for jj in range(4):
                j = jg * 4 + jj
                for tb in range(4):
                    zi = tb % 2
                    bcc, bcx, bcb = nbank(), nbank(), nbank()
                    for (slot, skey, b) in ((s_cc, k_cc, bcc), (s_cx, k_cx, bcx), (s_cb, k_cb, bcb)):
                        for k in R8:
                            mm(pa[b][:, :], ("pa", b), slot[:, k, jj * 128:(jj + 1) * 128], [skey],
                               X[:, k, tb * 512:(tb + 1) * 512], [("X", k, tb)], k == 0, k == 7)
                    ccs = scr[zi][:, 0:512]
                    t0 = scr[zi][:, 512:1024]
                    t1 = scr[2][:, zi * 512:(zi + 1) * 512]
                    S.op("act", lambda e, ccs=ccs, bcc=bcc: e.copy(out=ccs, in_=pa[bcc][:, :]), reads=[("pa", bcc)], writes=[SH(zi, 0)])
                    zcur = zb[zi]
                    zprev = zb[1 - zi]
                    zk = SK(3 + zi)
                    if tb == 0:
                        S.op("dve", lambda e, zcur=zcur, jj=jj: e.tensor_copy(out=zcur[:, 0:2], in_=zhalo[:, 8 + jj * 2: 8 + jj * 2 + 2]),
                             reads=["zh_b"], writes=zk)
                    else:
                        S.op("dve", lambda e, zcur=zcur, zprev=zprev: e.tensor_copy(out=zcur[:, 0:2], in_=zprev[:, 512:514]),
                             reads=SK(3 + 1 - zi), writes=zk)
                    S.op("dve", lambda e, zcur=zcur, ccs=ccs, bcx=bcx: e.tensor_tensor(out=zcur[:, 2:514], in0=pa[bcx][:, :], in1=ccs, op=ALU.mult),
                         reads=[("pa", bcx), SH(zi, 0)], writes=zk)
                    S.op("act", lambda e, zcur=zcur, t0=t0, j=j: e.activation(out=t0, in_=zcur[:, 0:512], func=AF.Copy,
                                                                             scale=convw[:, j * 3:j * 3 + 1]),
                         reads=zk + ["convw"], writes=[SH(zi, 1)])
                    S.op("dve", lambda e, zcur=zcur, t0=t0, t1=t1, j=j: e.scalar_tensor_tensor(out=t1, in0=zcur[:, 1:513], scalar=convw[:, j * 3 + 1:j * 3 + 2],
                                                                                               in1=t0, op0=ALU.mult, op1=ALU.add),
                         reads=zk + [SH(zi, 1), "convw"], writes=[SH(2, zi)])
                    S.op("dve", lambda e, zcur=zcur, t0=t0, t1=t1, j=j: e.scalar_tensor_tensor(out=t0, in0=zcur[:, 2:514], scalar=convw[:, j * 3 + 2:j * 3 + 3],
                                                                                               in1=t1, op0=ALU.mult, op1=ALU.add),
                         reads=zk + [SH(2, zi), "convw"], writes=[SH(zi, 1)])
                    S.op("dve", lambda e, t0=t0, bcb=bcb, j=j, tb=tb: e.tensor_tensor(out=Y[:, j, tb * 512:(tb + 1) * 512], in0=pa[bcb][:, :], in1=t0, op=ALU.mult),
                         reads=[("pa", bcb), SH(zi, 1)], writes=[("Y", j, tb)])
            ring_rel(3)
        dump("zcT", Y[:, :, :], [128, 8, 2048], BF16, XK("Y", R8, range(4)))
        if stop_after == "C":
            S.emit(nc, dump_ops)
            return nc

        def phase_Y(SRC, srcname, second):
            for jg in range(2):
                s_w, k_w = ring_use()
                s_g, k_g = ring_use()
                for jj in range(4):
                    j = jg * 4 + jj
                    for tb in range(4):
                        by, bg = nbank(), nbank()
                        for k in R8:
                            mm(pa[by][:, :], ("pa", by), s_w[:, k, jj * 128:(jj + 1) * 128], [k_w],
                               SRC[:, k, tb * 512:(tb + 1) * 512], [(srcname, k, tb)], k == 0, k == 7)
                        for k in R8:
                            mm(pa[bg][:, :], ("pa", bg), s_g[:, k, jj * 128:(jj + 1) * 128], [k_g],
                               X[:, k, tb * 512:(tb + 1) * 512], [("X", k, tb)], k == 0, k == 7)
                        si = tb % 2
                        sg = scr[0][:, si * 512:(si + 1) * 512]
                        S.op("act", lambda e, sg=sg, bg=bg: e.activation(out=sg, in_=pa[bg][:, :], func=AF.Sigmoid),
                             reads=[("pa", bg)], writes=[SH(0, si)])
                        dst = Z[:, j, tb * 512:(tb + 1) * 512]
                        if not second:
                            S.op("dve", lambda e, sg=sg, by=by, dst=dst: e.tensor_tensor(out=dst, in0=pa[by][:, :], in1=sg, op=ALU.mult),
                                 reads=[("pa", by), SH(0, si)], writes=[("Z", j, tb)])
                        else:
                            tm = scr[1][:, si * 512:(si + 1) * 512]
                            S.op("dve", lambda e, sg=sg, by=by, tm=tm: e.tensor_tensor(out=tm, in0=pa[by][:, :], in1=sg, op=ALU.mult),
                                 reads=[("pa", by), SH(0, si)], writes=[SH(1, si)])
                            S.op("pool", lambda e, tm=tm, dst=dst: e.tensor_tensor(out=dst, in0=tm, in1=dst, op=ALU.add),
                                 reads=[SH(1, si), ("Z", j, tb)], writes=[("Z", j, tb)])
                ring_rel(2)

        modvec(2, G_t, ["G"])
        phase_Y(Y, "Y", False)
        dump("pT", Z[:, :, :], [128, 8, 2048], BF16, XK("Z", R8, range(4)))
        if stop_after == "Y1":
            S.emit(nc, dump_ops)
            return nc

        s_q, k_q = ring_use()
        s_k, k_k = ring_use()
        s_v0, k_v0 = ring_use()
        s_v1, k_v1 = ring_use()
        qkT = [scr[2][:, 0:512].bitcast(BF16), scr[2][:, 512:1024].bitcast(BF16)]
        qh_b = [scr[3][:, 0:256].bitcast(BF16), scr[3][:, 512:768].bitcast(BF16)]
        kh_b = [scr[3][:, 256:512].bitcast(BF16), scr[3][:, 768:1024].bitcast(BF16)]
        P_b = [scr[4][:, 0:256].bitcast(BF16), scr[4][:, 256:512].bitcast(BF16)]
        gnw = S_t
        cload(gnw, gnw_d, ("AR", 2))
        gam = [1.0 - 2.0 ** (-5.0 - h) for h in range(4)]
        cdec = [float(g ** 128) for g in gam]
        qdec3 = qdec[:, :].unsqueeze(2).to_broadcast([128, 4, 128])
        kdec3 = kdec[:, :].unsqueeze(2).to_broadcast([128, 4, 128])

        def R_Ap(t, which):
            lst = {"qk": ((s_q, k_q, 0), (s_k, k_k, 1)), "v0": ((s_v0, k_v0, 2),), "v1": ((s_v1, k_v1, 3),)}[which]
            for (slot, skey, b) in lst:
                for k in R8:
                    mm(pa[b][:, :], ("pa", b), X[:, k, t * 128:(t + 1) * 128], [("X", k, t // 4)], slot[:, k, :], [skey], k == 0, k == 7)

        def R_Av(t, which):
            p2 = t % 2
            if which == "v0":
                S.op("act", lambda e: e.copy(out=vt_b[p2][:, 0:512], in_=pa[2][:, :]), reads=[("pa", 2)], writes=[("vt", p2)])
            else:
                S.op("act", lambda e: e.copy(out=vt_b[p2][:, 512:1024], in_=pa[3][:, :]), reads=[("pa", 3)], writes=[("vt", p2)])

        r_tails = {}

        def R_A3(t):
            tq, tk = r_tails.pop(t)
            tq()
            tk()

        def R_A2(t):
            p2 = t % 2

            def decq(Rr, rkey):
                S.op("pool", lambda e: e.tensor_tensor(out=qh_b[p2].rearrange("p (h d) -> p h d", h=4),
                                                       in0=Rr.rearrange("p (h d) -> p h d", h=4), in1=qdec3, op=ALU.mult),
                     reads=[rkey, "qdec"], writes=[SH(3, p2)])

            def deck(Rr, rkey):
                S.op("pool", lambda e: e.tensor_tensor(out=kh_b[p2].rearrange("p (h d) -> p h d", h=4),
                                                       in0=Rr.rearrange("p (h d) -> p h d", h=4), in1=kdec3, op=ALU.mult),
                     reads=[rkey, "kdec"], writes=[SH(3, p2)])
            tq = rotary(0, t, decq, 0, defer="dec")
            tk = rotary(1, t, deck, 1, defer="dec")
            r_tails[t] = (tq, tk)

        def R_B(t):
            p2 = t % 2
            pi = ctr["pt"] % 2
            ctr["pt"] += 1
            for h in range(4):
                S.op("pe", lambda e, h=h: e.transpose(pt[pi][:, h * 128:(h + 1) * 128], qh_b[p2][:, h * 128:(h + 1) * 128], ident[:, :]),
                     reads=[SH(3, p2), "ident"], writes=[("pt", pi)])
            for h in range(4):
                S.op("pe", lambda e, h=h: e.transpose(pt[pi][:, 512 + h * 128:512 + (h + 1) * 128], kh_b[p2][:, h * 128:(h + 1) * 128], ident[:, :]),
                     reads=[SH(3, p2), "ident"], writes=[("pt", pi)])
            S.op("act", lambda e: e.copy(out=qkT[p2], in_=pt[pi][:, :]), reads=[("pt", pi)], writes=[SH(2, p2)])

        def R_C(t):
            p2 = t % 2
            b = 4
            for h in range(4):
                mm(pa[b][:, h * 128:(h + 1) * 128], ("pa", b), qkT[p2][:, 512 + h * 128:512 + (h + 1) * 128], [SH(2, p2)],
                   qkT[p2][:, h * 128:(h + 1) * 128], [SH(2, p2)], True, True)
            S.op("dve", lambda e: e.tensor_tensor(out=P_b[p2], in0=pa[b][:, :], in1=mask[:, :], op=ALU.mult),
                 reads=[("pa", b), "mask"], writes=[SH(4, 0)])

        def R_D(t):
            p2 = t % 2
            bo = [5, 4]
            bd = [0, 1]
            on = scr[p2]
            for h in range(4):
                d_ap = pa[bd[h // 2]][:, (h % 2) * 256:(h % 2 + 1) * 256]
                mm(d_ap, ("pa", bd[h // 2]), kh_b[p2][:, h * 128:(h + 1) * 128], [SH(3, p2)],
                   vt_b[p2][:, h * 256:(h + 1) * 256], [("vt", p2)], True, True)
            for h in range(4):
                o_ap = pa[bo[h // 2]][:, (h % 2) * 256:(h % 2 + 1) * 256]
                mm(o_ap, ("pa", bo[h // 2]), P_b[p2][:, h * 128:(h + 1) * 128], [SH(4, 0)],
                   vt_b[p2][:, h * 256:(h + 1) * 256], [("vt", p2)], True, False)
                mm(o_ap, ("pa", bo[h // 2]), qkT[p2][:, h * 128:(h + 1) * 128], [SH(2, p2)],
                   state_bf[:, h * 256:(h + 1) * 256], [("state_bf", h)], False, True)
                if h % 2 == 1:
                    hb = h // 2
                    S.op("act", lambda e, hb=hb: e.copy(out=on[:, hb * 512:(hb + 1) * 512], in_=pa[bo[hb]][:, :]),
                         reads=[("pa", bo[hb])], writes=[SH(p2, hb)])
            for h in range(4):
                d_ap = pa[bd[h // 2]][:, (h % 2) * 256:(h % 2 + 1) * 256]
                st_ap = state[:, h * 256:(h + 1) * 256]
                S.op("dve", lambda e, h=h, d_ap=d_ap, st_ap=st_ap: e.scalar_tensor_tensor(out=st_ap, in0=st_ap, scalar=cdec[h], in1=d_ap,
                                                                                        op0=ALU.mult, op1=ALU.add),
                     reads=[("pa", bd[h // 2]), ("state", h)], writes=[("state", h)])
                if h % 2 == 1:
                    hb = h // 2
                    S.op("act", lambda e, hb=hb: e.copy(out=state_bf[:, hb * 512:(hb + 1) * 512], in_=state[:, hb * 512:(hb + 1) * 512]),
                         reads=[("state", 2 * hb), ("state", 2 * hb + 1)], writes=[("state_bf", 2 * hb), ("state_bf", 2 * hb + 1)])
            for h in range(4):
                o_sb = on[:, h * 256:(h + 1) * 256]
                S.op("dve", lambda e, h=h, o_sb=o_sb: e.bn_stats(out=bn6[:, p2, h, :], in_=o_sb), reads=[SH(p2, h // 2)], writes=[("bn6", p2, h)])
                S.op("dve", lambda e, h=h: e.bn_aggr(out=mv[:, p2, h, :], in_=bn6[:, p2, h, :]), reads=[("bn6", p2, h)], writes=[("mv", p2)])
            S.op("dve", lambda e: e.tensor_scalar(out=gve[:, p2, :], in0=mv[:, p2, :, 1], scalar1=EPS, scalar2=None, op0=ALU.add),
                 reads=[("mv", p2)], writes=[("gve", p2)])
            S.op("pool", lambda e: e.tensor_tensor(out=grs[:, p2, :], in0=gve[:, p2, :], in1=mhalf[:, :], op=ALU.pow),
                 reads=[("gve", p2), "mhalf"], writes=[("grs", p2)])
            S.op("dve", lambda e: e.scalar_tensor_tensor(out=gnm[:, p2, :], in0=mv[:, p2, :, 0], scalar=-1.0, in1=grs[:, p2, :],
                                                         op0=ALU.mult, op1=ALU.mult),
                 reads=[("mv", p2), ("grs", p2)], writes=[("gnm", p2)])
        def R_D2(t):
            p2 = t % 2
            on = scr[p2]
            for h in range(4):
                o_sb = on[:, h * 256:(h + 1) * 256]
                S.op("act", lambda e, h=h, o_sb=o_sb: e.activation(out=o_sb, in_=o_sb, func=AF.Identity,
                                                                   scale=grs[:, p2, h:h + 1], bias=gnm[:, p2, h:h + 1]),
                     reads=[SH(p2, h // 2), ("grs", p2), ("gnm", p2)], writes=[SH(p2, h // 2)])
            x3 = t % 3
            for hb in range(2):
                S.op("pool", lambda e, hb=hb: e.tensor_tensor(out=xs_b[x3][:, hb * 512:(hb + 1) * 512], in0=on[:, hb * 512:(hb + 1) * 512],
                                                             in1=gnw[:, hb * 512:(hb + 1) * 512], op=ALU.mult),
                     reads=[SH(p2, hb), ("AR", 2)], writes=[("xs", x3)])

        def R_E(t):
            x3 = t % 3
            pi = transpose8(xs_b[x3], ("xs", x3))
            S.op("act", lambda e: e.copy(out=Y[:, :, t * 128:(t + 1) * 128], in_=pt[pi][:, :].rearrange("p (k m) -> p k m", k=8)),
                 reads=[("pt", pi)], writes=XK("Y", R8, [t // 4]))

        R_Ap(0, "qk")
        R_A2(0)
        R_Ap(0, "v0")
        R_Av(0, "v0")
        R_Ap(0, "v1")
        R_Av(0, "v1")
        R_A3(0)
        for i in range(16):
            nx = i + 1 < 16
            if nx:
                R_Ap(i + 1, "qk")
                R_A2(i + 1)
            R_B(i)
            if nx:
                R_Ap(i + 1, "v0")
                R_Av(i + 1, "v0")
            if i >= 2:
                R_E(i - 2)
            R_C(i)
            if nx:
                R_Ap(i + 1, "v1")
                R_Av(i + 1, "v1")
                R_A3(i + 1)
            if i >= 1:
                R_D2(i - 1)
            R_D(i)
        R_D2(15)
        ring_rel(4)
        r2_groups = [0]

        for jg in range(2):
            s_g, k_g = ring_use()
            for jj in range(4):
                j = jg * 4 + jj
                for tb in range(4):
                    bg = nbank()
                    for k in R8:
                        mm(pa[bg][:, :], ("pa", bg), s_g[:, k, jj * 128:(jj + 1) * 128], [k_g],
                           X[:, k, tb * 512:(tb + 1) * 512], [("X", k, tb)], k == 0, k == 7)
                    si = tb % 2
                    sg = scr[0][:, si * 512:(si + 1) * 512]
                    S.op("act", lambda e, sg=sg, bg=bg: e.activation(out=sg, in_=pa[bg][:, :], func=AF.Silu),
                         reads=[("pa", bg)], writes=[SH(0, si)])
                    dst = Y[:, j, tb * 512:(tb + 1) * 512]
                    S.op("pool", lambda e, sg=sg, dst=dst: e.tensor_tensor(out=dst, in0=sg, in1=dst, op=ALU.mult),
                         reads=[SH(0, si), ("Y", j, tb)], writes=[("Y", j, tb)])
                    if r2_groups[0] == 0:
                        R_E(14)
                    elif r2_groups[0] == 1:
                        R_E(15)
                    r2_groups[0] += 1
            ring_rel(1)
        dump("zretT", Y[:, :, :], [128, 8, 2048], BF16, XK("Y", R8, range(4)))
        if stop_after == "R":
            S.emit(nc, dump_ops)
            return nc

        make_SB(4, 3, n2w_d)
        phase_Y(Y, "Y", True)
        dump("mergedT", Z[:, :, :], [128, 8, 2048], BF16, XK("Z", R8, range(4)))

        s_o0, k_o0 = ring_use()
        s_o1, k_o1 = ring_use()
        for n, (slot, skey) in enumerate(((s_o0, k_o0), (s_o1, k_o1))):
            gb = G_t[:, n * 512:(n + 1) * 512].unsqueeze(1).to_broadcast([128, 4, 512])
            for kh in range(2):
                S.op("dve" if kh == 0 else "pool",
                     lambda e, slot=slot, kh=kh, gb=gb: e.tensor_tensor(out=slot[:, kh * 4:(kh + 1) * 4, :], in0=slot[:, kh * 4:(kh + 1) * 4, :],
                                                                       in1=gb, op=ALU.mult),
                     reads=[skey, "G"], writes=[skey])
        pend = {}

        o_x = {}
        o_h = {}

        def O_load(t):
            o_x[t] = load_x(x_main, t)

        def O_mm(t):
            xi = o_x[t]
            b0, b1 = nbank(), nbank()
            for (slot, skey, b) in ((s_o0, k_o0, b0), (s_o1, k_o1, b1)):
                for k in R8:
                    mm(pa[b][:, :], ("pa", b), Z[:, k, t * 128:(t + 1) * 128], [("Z", k, t // 4)], slot[:, k, :], [skey], k == 0, k == 7)
            for n, b in ((0, b0), (1, b1)):
                S.op("dve", lambda e, n=n, b=b: e.tensor_tensor(out=scr[xi][:, n * 512:(n + 1) * 512], in0=pa[b][:, :],
                                                               in1=scr[xi][:, n * 512:(n + 1) * 512], op=ALU.add),
                     reads=[("pa", b), SH(xi, n)], writes=[SH(xi, n)])
            dma("act", h1s[t * 128:(t + 1) * 128, :], scr[xi][:, :], SK(xi), [("h1s", t)], ("h1st", t % 2))
            o_h[t] = nA(scr[xi][:, :], SK(xi))

        def O_T(t):
            xsi = pend[t]
            pi = transpose8(xs_b[xsi], ("xs", xsi))
            S.op("act", lambda e: e.copy(out=X[:, :, t * 128:(t + 1) * 128], in_=pt[pi][:, :].rearrange("p (k m) -> p k m", k=8)),
                 reads=[("pt", pi)], writes=XK("X", R8, [t // 4]))

        def O_iter(i):
            if 2 <= i < 18:
                pend[i - 2] = nC(o_h[i - 2], add_eng=("pool" if i % 2 == 0 else "dve"))
            if i < 16:
                O_mm(i)
            if 1 <= i < 17:
                nB(o_h[i - 1])
            if i + 1 < 16:
                O_load(i + 1)
            if 3 <= i < 19:
                O_T(i - 3)

        O_load(0)
        for i in range(16):
            O_iter(i)
        ring_rel(2)
        O_iter(16)

        modvec(5, G_t, ["G"])
        O_iter(17)
        fnw = xsT[:, 0:2048].bitcast(F32)
        f_groups = [0]
        h2a = AR[:, :].rearrange("p (t n) -> p t n", n=512)
        out_ops = []

        def hid_ap(f, c0, c1):
            buf = Y if f < 16 else Z
            ff = f % 16
            return buf[:, ff // 2, (ff % 2) * 1024 + c0:(ff % 2) * 1024 + c1]

        def hid_keys(f, tb2s):
            name = "Y" if f < 16 else "Z"
            ff = f % 16
            return [(name, ff // 2, (ff % 2) * 2 + tb2) for tb2 in tb2s]

        for mb in range(2):
            for fc in range(8):
                s_w, k_w = ring_use()
                for ff in range(4):
                    f = fc * 4 + ff
                    for tb2 in range(2):
                        tb = mb * 2 + tb2
                        b = nbank()
                        for k in R8:
                            mm(pa[b][:, :], ("pa", b), s_w[:, k, ff * 128:(ff + 1) * 128], [k_w],
                               X[:, k, tb * 512:(tb + 1) * 512], [("X", k, tb)], k == 0, k == 7)
                        ri = tb2
                        r = scr[6][:, ri * 512:(ri + 1) * 512]
                        S.op("act", lambda e, r=r, b=b: e.activation(out=r, in_=pa[b][:, :], func=AF.Relu),
                             reads=[("pa", b)], writes=[SH(6, ri)])
                        dst = hid_ap(f, tb2 * 512, (tb2 + 1) * 512)
                        S.op("pool", lambda e, r=r, dst=dst: e.tensor_tensor(out=dst, in0=r, in1=r, op=ALU.mult),
                             reads=[SH(6, ri)], writes=hid_keys(f, [tb2]))
                        f_groups[0] += 1
                        if f_groups[0] == 2:
                            O_iter(18)
                            dma("sp", fnw, fnw_d, [], ["fnw", ("xs", 0), ("xs", 1)], "fnw")
                ring_rel(1)
            for n in range(2):
                sl = [ring_use() for _ in range(4)]
                for tt in range(8):
                    t = mb * 8 + tt
                    b = nbank()
                    for f in range(32):
                        slot, skey = sl[f // 8]
                        mm(pa[b][:, :], ("pa", b), hid_ap(f, tt * 128, (tt + 1) * 128), hid_keys(f, [tt // 4]),
                           slot[:, f % 8, :], [skey], f == 0, f == 31)
                        if tt == 7 and f % 8 == 7:
                            ring_rel(1)
                    xi = load_x(h1s, t, reads=[("h1s", t)])
                    tmpb = scr[5][:, n * 512:(n + 1) * 512]
                    S.op("dve", lambda e, b=b, tmpb=tmpb, n=n: e.tensor_tensor(out=tmpb, in0=pa[b][:, :], in1=G_t[:, n * 512:(n + 1) * 512], op=ALU.mult),
                         reads=[("pa", b), "G"], writes=[SH(5, n)])
                    if n == 0:
                        S.op("pool", lambda e, tmpb=tmpb, xi=xi, tt=tt: e.tensor_tensor(out=h2a[:, tt, :], in0=tmpb, in1=scr[xi][:, 0:512], op=ALU.add),
                             reads=[SH(5, 0), SH(xi, 0)], writes=[("AR", tt // 2)])
                        continue
                    S.op("pool", lambda e, tmpb=tmpb, xi=xi: e.tensor_tensor(out=scr[xi][:, 512:1024], in0=tmpb, in1=scr[xi][:, 512:1024], op=ALU.add),
                         reads=[SH(5, 1), SH(xi, 1)], writes=[SH(xi, 1)])
                    S.op("pool", lambda e, xi=xi, tt=tt: e.tensor_copy(out=scr[xi][:, 0:512], in_=h2a[:, tt, :]),
                         reads=[("AR", tt // 2), SH(xi, 0)], writes=[SH(xi, 0)])
                    r_ap, rkey = norm_stats(scr[xi][:, :], SK(xi))
                    ctr["tmp"] += 1
                    oi = 3 + (ctr["tmp"] % 2)
                    S.op("dve", lambda e, xi=xi, oi=oi, r_ap=r_ap: e.scalar_tensor_tensor(out=scr[oi][:, :], in0=scr[xi][:, :], scalar=r_ap, in1=fnw,
                                                                                         op0=ALU.mult, op1=ALU.mult),
                         reads=SK(xi) + [rkey, "fnw"], writes=SK(oi))
                    out_ops.append(dma("act", y_out[t * 128:(t + 1) * 128, :], scr[oi][:, :], SK(oi), [("yout", t)], ("yst", t % 2)))

        S.emit(nc, list(out_ops) + dump_ops)
    return nc


def _consts(half):
    gam = np.array([1.0 - 2.0 ** (-5.0 - h) for h in range(4)], np.float64)
    s = np.arange(128)[:, None]
    c = np.arange(128)[None, :]
    same = (s // 64) == (c // 64)
    mask = np.zeros((128, 4, 128), np.float64)
    qdec = np.zeros((128, 4), np.float64)
    kdec = np.zeros((128, 4), np.float64)
    predec = np.zeros((128, 16, 4), np.float64)
    for h in range(4):
        g = gam[h]
        Dm = np.where(same, g ** np.abs(c - s), np.where(s < c, g ** (c - s).clip(0), 0.0))
        mask[:, h, :] = Dm * g ** (s - c - 128.0)
        qdec[:, h] = g ** (np.arange(128) + 1.0)
        kdec[:, h] = g ** (127.0 - np.arange(128)) * 128.0 ** -0.5
        tok = np.arange(16)[None, :] * 128 + np.arange(128)[:, None]
        predec[:, :, h] = (g ** (2047.0 - tok)) * 128.0 ** -0.5 * float(half)
    invf = (10000.0 ** (-np.arange(0, 128, 2, dtype=np.float32) / 128)).astype(np.float32)
    return dict(
        mask=mask.reshape(128, 512).astype(np.float32),
        qdec_rep=qdec.astype(np.float32),
        kdec_rep=kdec.astype(np.float32),
        predec=predec.reshape(128, 64).astype(np.float32),
        invf=np.ascontiguousarray(np.broadcast_to(invf[None, :], (128, 64))),
        flagcol=np.full((128, 1), float(half), np.float32),
        ident=np.eye(128, dtype=np.float32).astype(ml_dtypes.bfloat16),
    )


def make_in_maps(inputs):
    f32 = lambda a: np.ascontiguousarray(np.asarray(a, dtype=np.float32))
    x = f32(inputs["x"])
    c = f32(inputs["c"])
    pos = np.ascontiguousarray(np.asarray(inputs["positions"], dtype=np.int32))
    rep = lambda v: np.ascontiguousarray(np.broadcast_to(f32(v).reshape(1, -1), (128, f32(v).size)))
    shared = dict(
        w_ada=f32(inputs["w_ada"][0]), b_ada_rep=rep(inputs["b_ada"][0]),
        n1w_rep=rep(inputs["norm1_w"][0]), n2w_rep=rep(inputs["norm2_w"][0]),
        fnw_rep=rep(inputs["final_norm_w"]), gnw_rep=rep(inputs["ret_gn_w"][0]),
        w_in=f32(inputs["w_in"][0]), w_ro=f32(inputs["w_ret_out"][0]), w_co=f32(inputs["w_conv_out"][0]),
        w_o=f32(inputs["w_o"][0]), w_ff1=f32(inputs["w_ff1"][0]), w_ff2=f32(inputs["w_ff2"][0]),
        convw=np.ascontiguousarray(f32(inputs["conv_w"][0]).reshape(3, 8, 128).transpose(2, 1, 0).reshape(128, 24)),
    )
    maps = []
    for core in range(8):
        b, half = core // 2, core % 2
        m = dict(shared)
        m.update(_consts(half))
        m["x_main"] = np.ascontiguousarray(x[b, half * 2048:(half + 1) * 2048])
        m["x_pre"] = np.ascontiguousarray(x[b, 0:2048])
        m["cT"] = np.ascontiguousarray(c[b].reshape(8, 128).T)
        m["pos_m"] = np.ascontiguousarray(pos[b, half * 2048:(half + 1) * 2048].reshape(16, 128).T)
        m["pos_p"] = np.ascontiguousarray(pos[b, 0:2048].reshape(16, 128).T)
        maps.append(m)
    return maps


def kernel(**inputs):
    nc = build_program()
    maps = make_in_maps(inputs)
    res = run_bass_kernel_spmd(nc, maps, core_ids=list(range(8)))
    out = np.empty((4, 4096, 1024), np.float32)
    for core in range(8):
        b, half = core // 2, core % 2
        out[b, half * 2048:(half + 1) * 2048] = np.asarray(res.results[core]["y"], dtype=np.float32)
    return out
```

```python
import numpy as np
import ml_dtypes
import concourse.bass as bass
import concourse.mybir as mybir
from concourse.bass_utils import run_bass_kernel_spmd
from contextlib import ExitStack

F32 = mybir.dt.float32
BF16 = mybir.dt.bfloat16
I32 = mybir.dt.int32
AF = mybir.ActivationFunctionType
ALU = mybir.AluOpType

ENGS = ("pe", "act", "dve", "pool", "sp")
EPS = 1e-6
NSLOT = 5
TWO_PI = float(2.0 * np.pi)
PI = float(np.pi)


class Op:
    __slots__ = ("eng", "fn", "deps", "signal", "count", "dma_key", "dma_count", "idx")

    def __init__(self, eng, fn):
        self.eng = eng
        self.fn = fn
        self.deps = set()
        self.signal = False
        self.count = None
        self.dma_key = None
        self.dma_count = None


class Sched:
    def __init__(self):
        self.ops = {e: [] for e in ENGS}
        self.last_writer = {}
        self.readers = {}
        self.last_dma = {}
        self.dma_counts = {}

    def op(self, eng, fn, reads=(), writes=(), dma_key=None):
        o = Op(eng, fn)
        deps = set()
        for r in reads:
            w = self.last_writer.get(r)
            if w is not None:
                deps.add(w)
        for w_ in writes:
            lw = self.last_writer.get(w_)
            if lw is not None:
                deps.add(lw)
            for rd in self.readers.get(w_, ()):
                deps.add(rd)
        if dma_key is not None:
            o.dma_key = dma_key
            prev = self.last_dma.get(dma_key)
            if prev is not None:
                deps.add(prev)
            self.last_dma[dma_key] = o
            c = self.dma_counts.get(dma_key, 0) + 16
            self.dma_counts[dma_key] = c
            o.dma_count = c
        deps.discard(o)
        o.deps = deps
        for r in reads:
            self.readers.setdefault(r, []).append(o)
        for w_ in writes:
            self.last_writer[w_] = o
            self.readers[w_] = []
        self.ops[eng].append(o)
        return o

    def emit(self, nc, final_ops):
        def needs_sem(d, o):
            return not (d.eng == "pe" and o.eng == "pe" and d.dma_key is None and o.dma_key is None)

        for e in ENGS:
            for o in self.ops[e]:
                for d in o.deps:
                    if d.dma_key is None and needs_sem(d, o):
                        d.signal = True
        for e in ENGS:
            c = 0
            for o in self.ops[e]:
                if o.signal and o.dma_key is None:
                    c += 1
                    o.count = c
        dma_keys = list(self.dma_counts.keys())
        with ExitStack() as es:
            esem = {e: es.enter_context(nc.semaphore("s_" + e)) for e in ENGS}
            dsem = {k: es.enter_context(nc.semaphore("d_%d" % i)) for i, k in enumerate(dma_keys)}
            block = es.enter_context(nc.Block())

            def run_engine(e, eng):
                seen = {}
                for o in self.ops[e]:
                    need = {}
                    for d in o.deps:
                        if d.dma_key is not None:
                            key, val = ("d", d.dma_key), d.dma_count
                        else:
                            if not needs_sem(d, o):
                                continue
                            key, val = ("e", d.eng), d.count
                        if val > need.get(key, 0):
                            need[key] = val
                    for key, val in need.items():
                        if seen.get(key, 0) >= val:
                            continue
                        seen[key] = val
                        eng.wait_ge(dsem[key[1]] if key[0] == "d" else esem[key[1]], val)
                    ins = o.fn(eng)
                    if o.dma_key is not None:
                        ins.then_inc(dsem[o.dma_key], 16)
                    elif o.signal:
                        ins.then_inc(esem[e], 1)
                if e == "sp":
                    for o in final_ops:
                        eng.wait_ge(dsem[o.dma_key], o.dma_count)

            @block.tensor
            def _(eng):
                run_engine("pe", eng)

            @block.scalar
            def _(eng):
                run_engine("act", eng)

            @block.vector
            def _(eng):
                run_engine("dve", eng)

            @block.gpsimd
            def _(eng):
                run_engine("pool", eng)

            @block.sync
            def _(eng):
                run_engine("sp", eng)


def build_program(dumps=(), stop_after=None):
    nc = bass.Bass("TRN2", target_bir_lowering=False)
    S = Sched()

    def din(name, shape, dt=F32):
        return nc.dram_tensor(name, list(shape), dt, kind="ExternalInput").ap()

    x_main = din("x_main", [2048, 1024])
    x_pre = din("x_pre", [2048, 1024])
    cT_d = din("cT", [128, 8])
    posm_d = din("pos_m", [128, 16], I32)
    posp_d = din("pos_p", [128, 16], I32)
    w_ada = din("w_ada", [1024, 6144])
    b_ada_rep = din("b_ada_rep", [128, 6144])
    n1w_d = din("n1w_rep", [128, 1024])
    n2w_d = din("n2w_rep", [128, 1024])
    fnw_d = din("fnw_rep", [128, 1024])
    gnw_d = din("gnw_rep", [128, 1024])
    w_in = din("w_in", [1024, 8192])
    w_ro = din("w_ro", [1024, 1024])
    w_co = din("w_co", [1024, 1024])
    w_o = din("w_o", [1024, 1024])
    w_ff1 = din("w_ff1", [1024, 4096])
    w_ff2 = din("w_ff2", [4096, 1024])
    convw_d = din("convw", [128, 24])
    invf_d = din("invf", [128, 64])
    mask_d = din("mask", [128, 512])
    qdec_d = din("qdec_rep", [128, 4])
    kdec_d = din("kdec_rep", [128, 4])
    predec_d = din("predec", [128, 64])
    flag_d = din("flagcol", [128, 1])
    ident_d = din("ident", [128, 128], BF16)
    y_out = nc.dram_tensor("y", [2048, 1024], F32, kind="ExternalOutput").ap()
    h1s = nc.dram_tensor("h1s", [2048, 1024], F32, kind="Internal").ap()
    dump_ops = []

    with ExitStack() as es:
        def sb(name, shape, dt=F32):
            return es.enter_context(nc.sbuf_tensor(name, list(shape), dt))

        def ps(name, shape, dt=F32):
            return es.enter_context(nc.psum_tensor(name, list(shape), dt))

        X = sb("X", [128, 8, 2048], BF16)
        Y = sb("Y", [128, 8, 2048], BF16)
        Z = sb("Z", [128, 8, 2048], BF16)
        ring_t = [sb("ring%d" % i, [128, 8, 512], BF16) for i in range(NSLOT)]
        AR = sb("AR", [128, 4096], F32)
        cosT = AR[:, 0:1024].rearrange("p (t i) -> p t i", i=64)
        sinT = AR[:, 1024:2048].rearrange("p (t i) -> p t i", i=64)
        S_t = AR[:, 2048:3072]
        B_t = AR[:, 3072:4096]
        G_t = sb("G_t", [128, 1024], F32)
        scr = [sb("scr%d" % i, [128, 1024], F32) for i in range(7)]
        xsT = sb("xsT", [128, 3072], BF16)
        xs_b = [xsT[:, 0:1024], xsT[:, 1024:2048], xsT[:, 2048:3072]]
        pa = [ps("pa%d" % i, [128, 512], F32) for i in range(6)]
        pt = [ps("pt%d" % i, [128, 1024], BF16) for i in range(2)]
        ident = sb("ident_s", [128, 128], BF16)
        ones_bf = sb("ones_bf", [128, 128], BF16)
        cT = sb("cT_s", [128, 8], F32)
        csil = sb("csil", [128, 8], F32)
        crep = sb("crep", [128, 8, 128], BF16)
        posi = sb("posi", [128, 16], I32)
        posf = sb("posf", [128, 16], F32)
        invf = sb("invf_s", [128, 64], F32)
        mask = sb("mask_s", [128, 512], F32)
        qdec = sb("qdec_s", [128, 4], F32)
        kdec = sb("kdec_s", [128, 4], F32)
        predec = sb("predec_s", [128, 64], F32)
        flagc = sb("flag_s", [128, 1], F32)
        convw = sb("convw_s", [128, 24], F32)
        mhalf = sb("mhalf", [128, 4], F32)
        ss = sb("ss", [128, 96], F32)
        var = sb("var", [128, 96], F32)
        rstd = sb("rstd", [128, 96], F32)
        uTp = [Z[:, i, 0:1024].rearrange("p (k m) -> p k m", k=8) for i in range(2)]
        uTpK = [[("Z", i, 0), ("Z", i, 1)] for i in range(2)]
        uTh = sb("uTh", [128, 8, 2], BF16)
        zhalo = sb("zhalo", [128, 16], F32)
        state = sb("state", [128, 1024], F32)
        state_bf = sb("state_bf", [128, 1024], BF16)
        vt_b = [sb("vt%d" % i, [128, 1024], BF16) for i in range(2)]
        ktp = [Z[:, 2 + i, 0:512] for i in range(2)]
        bn6 = sb("bn6", [128, 2, 4, 6], F32)
        mv = sb("mv", [128, 2, 4, 2], F32)
        gve = sb("gve", [128, 2, 4], F32)
        grs = sb("grs", [128, 2, 4], F32)
        gnm = sb("gnm", [128, 2, 4], F32)

        R8 = range(8)

        def SK(i):
            return [("scr", i, 0), ("scr", i, 1)]

        def SH(i, h):
            return ("scr", i, h)

        def XK(name, ks, tbs):
            return [(name, k, tb) for k in ks for tb in tbs]

        def dma(q, out, in_, reads, writes, key):
            return S.op(q, lambda e: e.dma_start(out=out, in_=in_), reads=reads, writes=writes, dma_key=key)

        def dump(name, ap, shape, dt, reads):
            if name not in dumps:
                return
            d = nc.dram_tensor("dbg_" + name, list(shape), dt, kind="ExternalOutput").ap()
            dump_ops.append(dma("sp", d, ap, reads, [], "dbg_" + name))

        def wchunk(w, r0, c0):
            return w[r0:r0 + 1024, c0:c0 + 512].rearrange("(kc p) n -> p kc n", p=128)

        chunks = []
        for v in (1, 0):
            chunks += [wchunk(w_ada, 0, v * 1024 + n * 512) for n in range(2)]
        chunks += [wchunk(w_in, 0, 512), wchunk(w_in, 0, 1024), wchunk(w_in, 0, 1536)]
        for jg in range(2):
            chunks += [wchunk(w_in, 0, 4096 + jg * 512), wchunk(w_in, 0, 5120 + jg * 512),
                       wchunk(w_in, 0, 3072 + jg * 512)]
        chunks += [wchunk(w_ada, 0, 2 * 1024 + n * 512) for n in range(2)]
        for jg in range(2):
            chunks += [wchunk(w_co, 0, jg * 512), wchunk(w_in, 0, 7168 + jg * 512)]
        chunks += [wchunk(w_in, 0, 0), wchunk(w_in, 0, 512), wchunk(w_in, 0, 1024), wchunk(w_in, 0, 1536)]
        chunks += [wchunk(w_in, 0, 2048), wchunk(w_in, 0, 2560)]
        for v in (4, 3):
            chunks += [wchunk(w_ada, 0, v * 1024 + n * 512) for n in range(2)]
        for jg in range(2):
            chunks += [wchunk(w_ro, 0, jg * 512), wchunk(w_in, 0, 6144 + jg * 512)]
        chunks += [wchunk(w_o, 0, 0), wchunk(w_o, 0, 512)]
        chunks += [wchunk(w_ada, 0, 5 * 1024 + n * 512) for n in range(2)]
        for mb in range(2):
            chunks += [wchunk(w_ff1, 0, fc * 512) for fc in range(8)]
            for n in range(2):
                chunks += [wchunk(w_ff2, fg * 1024, n * 512) for fg in range(4)]

        rstate = {"load": 0, "use": 0, "rel": 0}

        def ring_fill():
            while rstate["load"] < len(chunks) and rstate["load"] - rstate["rel"] < NSLOT:
                i = rstate["load"]
                s = i % NSLOT
                dma("pool", ring_t[s][:, :, :], chunks[i], [], [("ring", s)], ("ring", s))
                rstate["load"] += 1

        def ring_use():
            i = rstate["use"]
            assert i < rstate["load"], "ring underflow (too many resident chunks)"
            rstate["use"] += 1
            return ring_t[i % NSLOT], ("ring", i % NSLOT)

        def ring_rel(n=1):
            rstate["rel"] += n
            assert rstate["rel"] <= rstate["use"]
            ring_fill()

        bank_ctr = [0]

        def nbank():
            b = bank_ctr[0] % 6
            bank_ctr[0] += 1
            return b

        def mm(out, okey, lhsT, lkeys, rhs, rkeys, start, stop):
            S.op("pe", lambda e: e.matmul(out, lhsT, rhs, start=start, stop=stop),
                 reads=list(lkeys) + list(rkeys), writes=[okey])

        def cload(t, d, key):
            dma("sp", t, d, [], [key], key)

        cload(cT[:, :], cT_d, "cT")
        cload(ident[:, :], ident_d, "ident")
        ring_fill()
        cload(posi[:, :], posp_d, "posi")
        cload(invf[:, :], invf_d, "invf")
        cload(predec[:, :], predec_d, "predec")
        cload(kdec[:, :], kdec_d, "kdec")
        cload(qdec[:, :], qdec_d, "qdec")
        cload(mask[:, :], mask_d, "mask")
        cload(flagc[:, :], flag_d, "flagc")
        cload(convw[:, :], convw_d, "convw")
        S.op("dve", lambda e: e.memset(ones_bf[:, :], 1.0), writes=["ones"])
        S.op("dve", lambda e: e.memset(mhalf[:, :], -0.5), writes=["mhalf"])
        S.op("act", lambda e: e.activation(out=csil[:, :], in_=cT[:, :], func=AF.Silu), reads=["cT"], writes=["csil"])
        for k in R8:
            S.op("dve", lambda e, k=k: e.tensor_scalar(out=crep[:, k, :], in0=ones_bf[:, :], scalar1=csil[:, k:k + 1],
                                                      scalar2=None, op0=ALU.mult),
                 reads=["ones", "csil"], writes=[("crep", k)])

        def build_tables(defer=False):
            S.op("dve", lambda e: e.tensor_copy(out=posf[:, :], in_=posi[:, :]), reads=["posi"], writes=["posf"])
            ang = scr[5][:, :].rearrange("p (t i) -> p t i", i=64)
            tmpi = scr[6][:, :].bitcast(I32)
            angf = scr[5][:, :]
            fx = scr[6][:, :]
            for t in range(16):
                S.op("dve", lambda e, t=t: e.tensor_scalar(out=ang[:, t, :], in0=invf[:, :], scalar1=posf[:, t:t + 1],
                                                          scalar2=None, op0=ALU.mult),
                     reads=["posf", "invf"], writes=SK(5))
            acts = []
            for which, shift, dst, bi in (("sin", 0.0, AR[:, 1024:2048], 4), ("cos", PI / 2, AR[:, 0:1024], 3)):
                tmpf = scr[bi][:, :]
                BK = SK(bi)
                S.op("dve", lambda e, shift=shift, tmpf=tmpf: e.tensor_scalar(out=tmpf, in0=angf, scalar1=shift, scalar2=1.0 / TWO_PI,
                                                                               op0=ALU.add, op1=ALU.mult),
                     reads=SK(5), writes=BK)
                S.op("dve", lambda e, tmpf=tmpf: e.tensor_copy(out=tmpi, in_=tmpf), reads=BK, writes=SK(6))
                S.op("dve", lambda e, tmpf=tmpf: e.tensor_copy(out=tmpf, in_=tmpi), reads=SK(6), writes=BK)
                S.op("dve", lambda e, tmpf=tmpf: e.scalar_tensor_tensor(out=tmpf, in0=tmpf, scalar=-TWO_PI, in1=angf,
                                                                        op0=ALU.mult, op1=ALU.add),
                     reads=SK(5) + BK, writes=BK)
                S.op("dve", lambda e, shift=shift, tmpf=tmpf: e.tensor_scalar(out=fx, in0=tmpf, scalar1=PI - shift, scalar2=-TWO_PI,
                                                                               op0=ALU.is_gt, op1=ALU.mult),
                     reads=BK, writes=SK(6))
                S.op("dve", lambda e, shift=shift, tmpf=tmpf: e.scalar_tensor_tensor(out=tmpf, in0=tmpf, scalar=shift, in1=fx,
                                                                                      op0=ALU.add, op1=ALU.add),
                     reads=BK + SK(6), writes=BK)
                S.op("dve", lambda e, tmpf=tmpf: e.tensor_scalar(out=tmpf, in0=tmpf, scalar1=PI, scalar2=-PI, op0=ALU.min, op1=ALU.max),
                     reads=BK, writes=BK)
                key = ("AR", 1 if which == "sin" else 0)
                acts.append((dst, tmpf, BK, key))

            def do_acts():
                for dst, tmpf, BK, key in acts:
                    S.op("act", lambda e, dst=dst, tmpf=tmpf: e.activation(out=dst, in_=tmpf, func=AF.Sin), reads=BK, writes=[key])
            if defer:
                return do_acts
            do_acts()
            return None

        tables_act = build_tables(defer=True)

        btmp = scr[6][:, 512:1024]

        def modvec(v, out_ap, outkeys):
            for n in range(2):
                slot, skey = ring_use()
                dma("sp", btmp, b_ada_rep[:, v * 1024 + n * 512: v * 1024 + (n + 1) * 512], [], [SH(6, 1)], "bt")
                b = nbank()
                for k in R8:
                    mm(pa[b][:, :], ("pa", b), crep[:, k, :], [("crep", k)], slot[:, k, :], [skey], k == 0, k == 7)
                ring_rel()
                S.op("dve", lambda e, b=b, n=n: e.tensor_tensor(out=out_ap[:, n * 512:(n + 1) * 512], in0=pa[b][:, :],
                                                               in1=btmp, op=ALU.add),
                     reads=[("pa", b), SH(6, 1)], writes=outkeys)

        def make_SB(v_sc, v_sh, nw_d):
            modvec(v_sc, scr[3], SK(3))
            dma("sp", scr[4][:, :], nw_d, [], SK(4), "nw")
            S.op("dve", lambda e: e.scalar_tensor_tensor(out=S_t, in0=scr[3][:, :], scalar=1.0, in1=scr[4][:, :],
                                                         op0=ALU.add, op1=ALU.mult),
                 reads=SK(3) + SK(4), writes=[("AR", 2)])
            modvec(v_sh, B_t, [("AR", 3)])

        dump("S1", S_t, [128, 1024], F32, [("AR", 2)])
        dump("B1", B_t, [128, 1024], F32, [("AR", 3)])
        dump("cos", AR[:, 0:1024], [128, 1024], F32, [("AR", 0)])
        dump("sin", AR[:, 1024:2048], [128, 1024], F32, [("AR", 1)])

        ctr = {"xt": 0, "tmp": 0, "xs": 0, "col": 0, "pt": 0}

        def norm_stats(xt_ap, xkeys):
            col = ctr["col"]
            ctr["col"] += 1
            ti = 3 + ctr["tmp"] % 2
            S.op("act", lambda e: e.activation(out=scr[ti][:, :], in_=xt_ap, func=AF.Square, accum_out=ss[:, col:col + 1]),
                 reads=xkeys, writes=SK(ti) + [("ss", col)])
            S.op("dve", lambda e: e.tensor_scalar(out=var[:, col:col + 1], in0=ss[:, col:col + 1], scalar1=1.0 / 1024,
                                                  scalar2=EPS, op0=ALU.mult, op1=ALU.add),
                 reads=[("ss", col)], writes=[("var", col)])
            S.op("pool", lambda e: e.tensor_tensor(out=rstd[:, col:col + 1], in0=var[:, col:col + 1], in1=mhalf[:, 0:1], op=ALU.pow),
                 reads=[("var", col), "mhalf"], writes=[("rstd", col)])
            return rstd[:, col:col + 1], ("rstd", col)

        def norm_mod(xt_ap, xkeys, add_eng="pool"):
            r_ap, rkey = norm_stats(xt_ap, xkeys)
            ti = 3 + ctr["tmp"] % 2
            ctr["tmp"] += 1
            xi = ctr["xs"] % 3
            ctr["xs"] += 1
            S.op("dve", lambda e: e.scalar_tensor_tensor(out=scr[ti][:, :], in0=xt_ap, scalar=r_ap, in1=S_t,
                                                         op0=ALU.mult, op1=ALU.mult),
                 reads=list(xkeys) + [rkey, ("AR", 2)], writes=SK(ti))
            S.op(add_eng, lambda e: e.tensor_tensor(out=xs_b[xi], in0=scr[ti][:, :], in1=B_t, op=ALU.add),
                 reads=SK(ti) + [("AR", 3)], writes=[("xs", xi)])
            return xi

        def nA(xt_ap, xkeys):
            col = ctr["col"]
            ctr["col"] += 1
            S.op("act", lambda e: e.activation(out=Y[:, 7, 0:1024], in_=xt_ap, func=AF.Square, accum_out=ss[:, col:col + 1]),
                 reads=xkeys, writes=[("Y", 7, 0), ("Y", 7, 1), ("ss", col)])
            return dict(col=col, x=xt_ap, xk=list(xkeys))

        def nB(h):
            col = h["col"]
            S.op("dve", lambda e: e.tensor_scalar(out=var[:, col:col + 1], in0=ss[:, col:col + 1], scalar1=1.0 / 1024,
                                                  scalar2=EPS, op0=ALU.mult, op1=ALU.add),
                 reads=[("ss", col)], writes=[("var", col)])
            S.op("pool", lambda e: e.tensor_tensor(out=rstd[:, col:col + 1], in0=var[:, col:col + 1], in1=mhalf[:, 0:1], op=ALU.pow),
                 reads=[("var", col), "mhalf"], writes=[("rstd", col)])

        def nC2(h, tmp_ap, tmp_keys, xs_ap, xs_keys, add_eng):
            col = h["col"]
            S.op("dve", lambda e: e.scalar_tensor_tensor(out=tmp_ap, in0=h["x"], scalar=rstd[:, col:col + 1], in1=S_t,
                                                         op0=ALU.mult, op1=ALU.mult),
                 reads=h["xk"] + [("rstd", col), ("AR", 2)], writes=tmp_keys)
            S.op(add_eng, lambda e: e.tensor_tensor(out=xs_ap, in0=tmp_ap, in1=B_t, op=ALU.add),
                 reads=tmp_keys + [("AR", 3)], writes=xs_keys)

        def nC(h, add_eng="dve"):
            col = h["col"]
            ti = 3 + ctr["tmp"] % 2
            ctr["tmp"] += 1
            xi = ctr["xs"] % 3
            ctr["xs"] += 1
            S.op("dve", lambda e: e.scalar_tensor_tensor(out=scr[ti][:, :], in0=h["x"], scalar=rstd[:, col:col + 1], in1=S_t,
                                                         op0=ALU.mult, op1=ALU.mult),
                 reads=h["xk"] + [("rstd", col), ("AR", 2)], writes=SK(ti))
            S.op(add_eng, lambda e: e.tensor_tensor(out=xs_b[xi], in0=scr[ti][:, :], in1=B_t, op=ALU.add),
                 reads=SK(ti) + [("AR", 3)], writes=[("xs", xi)])
            return xi

        def transpose8(src_bf, srckey):
            pi = ctr["pt"] % 2
            ctr["pt"] += 1
            for k in R8:
                S.op("pe", lambda e, k=k: e.transpose(pt[pi][:, k * 128:(k + 1) * 128], src_bf[:, k * 128:(k + 1) * 128], ident[:, :]),
                     reads=[srckey, "ident"], writes=[("pt", pi)])
            return pi

        def load_x(src, t, reads=()):
            xi = ctr["xt"] % 3
            ctr["xt"] += 1
            dma("sp", scr[xi][:, :], src[t * 128:(t + 1) * 128, :], list(reads), SK(xi), ("xt", xi))
            return xi

        def rotary(b, t, dec_fn, which=0, defer=False):
            xv = pa[b][:, :].rearrange("p (h two i) -> p h two i", h=4, two=2)
            if which == 0:
                A, Bq, ka, kb = scr[5][:, 0:512], scr[5][:, 512:1024], SH(5, 0), SH(5, 1)
            else:
                A, Bq, ka, kb = scr[4][:, 512:1024], scr[6][:, 0:512], SH(4, 1), SH(6, 0)
            A4 = A.rearrange("p (h two i) -> p h two i", h=4, two=2)
            B4 = Bq.rearrange("p (h two i) -> p h two i", h=4, two=2)
            cos4 = cosT[:, t, :].unsqueeze(1).unsqueeze(1).to_broadcast([128, 4, 2, 64])
            sin3 = sinT[:, t, :].unsqueeze(1).to_broadcast([128, 4, 64])
            S.op("dve", lambda e: e.tensor_tensor(out=A4, in0=xv, in1=cos4, op=ALU.mult),
                 reads=[("pa", b), ("AR", 0)], writes=[ka])
            S.op("dve", lambda e: e.scalar_tensor_tensor(out=B4[:, :, 0, :], in0=xv[:, :, 1, :], scalar=-1.0, in1=sin3,
                                                         op0=ALU.mult, op1=ALU.mult),
                 reads=[("pa", b), ("AR", 1)], writes=[kb])
            S.op("dve", lambda e: e.tensor_tensor(out=B4[:, :, 1, :], in0=xv[:, :, 0, :], in1=sin3, op=ALU.mult),
                 reads=[("pa", b), ("AR", 1)], writes=[kb])
            def padd():
                S.op("pool", lambda e: e.tensor_tensor(out=A, in0=A, in1=Bq, op=ALU.add),
                     reads=[ka, kb], writes=[ka])

            def tail():
                padd()
                dec_fn(A, ka)
            if defer == "dec":
                padd()
                return lambda: dec_fn(A, ka)
            if defer:
                return tail
            tail()
            return None

        st_banks = (4, 5)
        pslots = {}
        pre_dec = {}

        def pre_proj(t):
            ui = t % 2
            bk, bv0, bv1 = (0, 1, 2) if t % 2 == 0 else (3, 1, 2)
            for (slot, skey, b) in ((pslots["k"] + (bk,)), (pslots["v0"] + (bv0,)), (pslots["v1"] + (bv1,))):
                for k in R8:
                    mm(pa[b][:, :], ("pa", b), uTp[ui][:, k, :], uTpK[ui], slot[:, k, :], [skey], k == 0, k == 7)

            def dec_fn(Rr, rkey):
                pd3 = predec[:, t * 4:(t + 1) * 4].unsqueeze(2).to_broadcast([128, 4, 128])
                S.op("pool", lambda e: e.tensor_tensor(out=ktp[ui].rearrange("p (h d) -> p h d", h=4),
                                                       in0=Rr.rearrange("p (h d) -> p h d", h=4), in1=pd3, op=ALU.mult),
                     reads=[rkey, "predec"], writes=[("Z", 2 + ui, 0)])
            pre_dec[t] = rotary(bk, t, dec_fn, 0, defer="dec")
            S.op("act", lambda e: e.copy(out=vt_b[ui][:, 0:512], in_=pa[bv0][:, :]), reads=[("pa", bv0)], writes=[("vt", ui)])
            S.op("act", lambda e: e.copy(out=vt_b[ui][:, 512:1024], in_=pa[bv1][:, :]), reads=[("pa", bv1)], writes=[("vt", ui)])

        S.op("dve", lambda e: e.memset(state[:, :], 0.0), writes=[("state", h) for h in range(4)])

        def pre_state(t):
            ui = t % 2
            for h in range(4):
                b = st_banks[h // 2]
                mm(pa[b][:, (h % 2) * 256:(h % 2 + 1) * 256], ("pa", b), ktp[ui][:, h * 128:(h + 1) * 128], [("Z", 2 + ui, 0)],
                   vt_b[ui][:, h * 256:(h + 1) * 256], [("vt", ui)], True, True)
            for h2 in range(2):
                b = st_banks[h2]
                S.op("dve", lambda e, b=b, h2=h2: e.tensor_tensor(out=state[:, h2 * 512:(h2 + 1) * 512], in0=state[:, h2 * 512:(h2 + 1) * 512],
                                                                 in1=pa[b][:, :], op=ALU.add),
                     reads=[("pa", b), ("state", 2 * h2), ("state", 2 * h2 + 1)], writes=[("state", 2 * h2), ("state", 2 * h2 + 1)])

        pre_h = {}
        pre_xs = {}

        def pre_T(t):
            xsi = pre_xs[t]
            pi = transpose8(xs_b[xsi], ("xs", xsi))
            ui = t % 2
            S.op("act", lambda e: e.copy(out=uTp[ui], in_=pt[pi][:, :].rearrange("p (k m) -> p k m", k=8)),
                 reads=[("pt", pi)], writes=uTpK[ui])
            if t == 15:
                S.op("dve", lambda e: e.tensor_copy(out=uTh[:, :, :], in_=uTp[ui][:, :, 126:128]), reads=uTpK[ui], writes=["uTh"])

        YK = lambda k: [("Y", k, tb) for tb in range(4)]
        mx = [Y[:, i, :].bitcast(F32) for i in range(3)]
        mtmp = [Y[:, 3 + i, :].bitcast(F32) for i in range(2)]
        mxs = [Y[:, 5, 0:1024], Y[:, 5, 1024:2048], Y[:, 6, 0:1024]]
        mxsK = [[("Y", 5, 0), ("Y", 5, 1)], [("Y", 5, 2), ("Y", 5, 3)], [("Y", 6, 0), ("Y", 6, 1)]]
        m_h = {}

        def main_T(t):
            x3 = t % 3
            pi = transpose8(mxs[x3], mxsK[x3][0])
            S.op("act", lambda e: e.copy(out=X[:, :, t * 128:(t + 1) * 128], in_=pt[pi][:, :].rearrange("p (k m) -> p k m", k=8)),
                 reads=[("pt", pi)] + mxsK[x3], writes=XK("X", R8, [t // 4]))

        for it in range(-4, 17):
            if 0 <= it + 4 < 16:
                t4 = it + 4
                xi = load_x(x_pre, t4)
                pre_h[t4] = nA(scr[xi][:, :], SK(xi))
                dma("sp", mx[t4 % 3], x_main[t4 * 128:(t4 + 1) * 128, :], [], YK(t4 % 3), ("mx", t4 % 3))
                m_h[t4] = nA(mx[t4 % 3], YK(t4 % 3))
            if 0 <= it + 3 < 16:
                nB(pre_h[it + 3])
                nB(m_h[it + 3])
            if it == -2:
                tables_act()
                make_SB(1, 0, n1w_d)
                pslots["k"] = ring_use()
                pslots["v0"] = ring_use()
                pslots["v1"] = ring_use()
            if 0 <= it + 2 < 16:
                t2 = it + 2
                pre_xs[t2] = nC(pre_h[t2])
                nC2(m_h[t2], mtmp[t2 % 2], YK(3 + t2 % 2), mxs[t2 % 3], mxsK[t2 % 3], "pool")
            if 0 <= it - 1 < 16:
                pre_dec.pop(it - 1)()
            if 0 <= it + 1 < 16:
                pre_T(it + 1)
                main_T(it + 1)
            if 0 <= it < 16:
                pre_proj(it)
                if it == 15:
                    ring_rel(3)
            if 0 <= it - 1 < 16:
                pre_state(it - 1)
        S.op("act", lambda e: e.copy(out=state_bf[:, :], in_=state[:, :]),
             reads=[("state", h) for h in range(4)], writes=[("state_bf", h) for h in range(4)])
        dump("state", state[:, :], [128, 1024], F32, [("state", h) for h in range(4)])
        dump("uT", X[:, :, :], [128, 8, 2048], BF16, XK("X", R8, range(4)))
        cload(posi[:, :], posm_d, "posi")
        build_tables()
        if stop_after == "U":
            S.emit(nc, dump_ops)
            return nc

        zb = [scr[3][:, 0:514], scr[4][:, 0:514]]
        for jg in range(2):
            s_cc, k_cc = ring_use()
            s_cx, k_cx = ring_use()
            s_cb, k_cb = ring_use()
            bh = nbank()
            for jj in range(4):
                for (slot, skey, off) in ((s_cc, k_cc, 0), (s_cx, k_cx, 8)):
                    for k in R8:
                        mm(pa[bh][:, off + jj * 2: off + jj * 2 + 2], ("pa", bh), slot[:, k, jj * 128:(jj + 1) * 128], [skey],
                           uTh[:, k, :], ["uTh"], k == 0, k == 7)
            S.op("act", lambda e, bh=bh: e.copy(out=zhalo[:, 0:8], in_=pa[bh][:, 0:8]), reads=[("pa", bh)], writes=["zh_a"])
            S.op("dve", lambda e, bh=bh: e.scalar_tensor_tensor(out=zhalo[:, 8:16], in0=pa[bh][:, 8:16], scalar=flagc[:, 0:1],
                                                               in1=zhalo[:, 0:8], op0=ALU.mult, op1=ALU.mult),
                 reads=[("pa", bh), "zh_a", "flagc"], writes=["zh_b"])
            for jj in range(4):
                j = jg * 4 + jj
                for tb in range(4):
                    zi = tb % 2
                    bcc, bcx, bcb = nbank(), nbank(), nbank()
                    for (slot, skey, b) in ((s_cc, k_cc, bcc), (s_cx, k_cx, bcx), (s_cb, k_cb, bcb)):
                        for k in R8:
                            mm(pa[b][:, :], ("pa", b), slot[:, k, jj * 128:(jj + 1) * 128], [skey],
                               X[:, k, tb * 512:(tb + 1) * 512], [("X", k, tb)], k == 0, k == 7)
                    ccs = scr[zi][:, 0:512]
                    t0 = scr[zi][:, 512:1024]
                    t1 = scr[2][:, zi * 512:(zi + 1) * 512]
                    S.op("act", lambda e, ccs=ccs, bcc=bcc: e.copy(out=ccs, in_=pa[bcc][:, :]), reads=[("pa", bcc)], writes=[SH(zi, 0)])
                    zcur = zb[zi]
                    zprev = zb[1 - zi]
                    zk = SK(3 + zi)
                    if tb == 0:
                        S.op("dve", lambda e, zcur=zcur, jj=jj: e.tensor_copy(out=zcur[:, 0:2], in_=zhalo[:, 8 + jj * 2: 8 + jj * 2 + 2]),
                             reads=["zh_b"], writes=zk)
                    else:
                        S.op("dve", lambda e, zcur=zcur, zprev=zprev: e.tensor_copy(out=zcur[:, 0:2], in_=zprev[:, 512:514]),
                             reads=SK(3 + 1 - zi), writes=zk)
                    S.op("dve", lambda e, zcur=zcur, ccs=ccs, bcx=bcx: e.tensor_tensor(out=zcur[:, 2:514], in0=pa[bcx][:, :], in1=ccs, op=ALU.mult),
                         reads=[("pa", bcx), SH(zi, 0)], writes=zk)
                    S.op("act", lambda e, zcur=zcur, t0=t0, j=j: e.activation(out=t0, in_=zcur[:, 0:512], func=AF.Copy,
                                                                             scale=convw[:, j * 3:j * 3 + 1]),
                         reads=zk + ["convw"], writes=[SH(zi, 1)])
                    S.op("dve", lambda e, zcur=zcur, t0=t0, t1=t1, j=j: e.scalar_tensor_tensor(out=t1, in0=zcur[:, 1:513], scalar=convw[:, j * 3 + 1:j * 3 + 2],
                                                                                               in1=t0, op0=ALU.mult, op1=ALU.add),
                         reads=zk + [SH(zi, 1), "convw"], writes=[SH(2, zi)])
                    S.op("dve", lambda e, zcur=zcur, t0=t0, t1=t1, j=j: e.scalar_tensor_tensor(out=t0, in0=zcur[:, 2:514], scalar=convw[:, j * 3 + 2:j * 3 + 3],
                                                                                               in1=t1, op0=ALU.mult, op1=ALU.add),
                         reads=zk + [SH(2, zi), "convw"], writes=[SH(zi, 1)])
                    S.op("dve", lambda e, t0=t0, bcb=bcb, j=j, tb=tb: e.tensor_tensor(out=Y[:, j, tb * 512:(tb + 1) * 512], in0=pa[bcb][:, :], in1=t0, op=ALU.mult),
                         reads=[("pa", bcb), SH(zi, 1)], writes=[("Y", j, tb)])
            ring_rel(3)
        dump("zcT", Y[:, :, :], [128, 8, 2048], BF16, XK("Y", R8, range(4)))
        if stop_after == "C":
            S.emit(nc, dump_ops)
            return nc

        def phase_Y(SRC, srcname, second):
            for jg in range(2):
                s_w, k_w = ring_use()
                s_g, k_g = ring_use()
                for jj in range(4):
                    j = jg * 4 + jj
                    for tb in range(4):
                        by, bg = nbank(), nbank()
                        for k in R8:
                            mm(pa[by][:, :], ("pa", by), s_w[:, k, jj * 128:(jj + 1) * 128], [k_w],
                               SRC[:, k, tb * 512:(tb + 1) * 512], [(srcname, k, tb)], k == 0, k == 7)
                        for k in R8:
                            mm(pa[bg][:, :], ("pa", bg), s_g[:, k, jj * 128:(jj + 1) * 128], [k_g],
                               X[:, k, tb * 512:(tb + 1) * 512], [("X", k, tb)], k == 0, k == 7)
                        si = tb % 2
                        sg = scr[0][:, si * 512:(si + 1) * 512]
                        S.op("act", lambda e, sg=sg, bg=bg: e.activation(out=sg, in_=pa[bg][:, :], func=AF.Sigmoid),
                             reads=[("pa", bg)], writes=[SH(0, si)])
                        dst = Z[:, j, tb * 512:(tb + 1) * 512]
                        if not second:
                            S.op("dve", lambda e, sg=sg, by=by, dst=dst: e.tensor_tensor(out=dst, in0=pa[by][:, :], in1=sg, op=ALU.mult),
                                 reads=[("pa", by), SH(0, si)], writes=[("Z", j, tb)])
                        else:
                            tm = scr[1][:, si * 512:(si + 1) * 512]
                            S.op("dve", lambda e, sg=sg, by=by, tm=tm: e.tensor_tensor(out=tm, in0=pa[by][:, :], in1=sg, op=ALU.mult),
                                 reads=[("pa", by), SH(0, si)], writes=[SH(1, si)])
                            S.op("pool", lambda e, tm=tm, dst=dst: e.tensor_tensor(out=dst, in0=tm, in1=dst, op=ALU.add),
                                 reads=[SH(1, si), ("Z", j, tb)], writes=[("Z", j, tb)])
                ring_rel(2)

        modvec(2, G_t, ["G"])
        phase_Y(Y, "Y", False)
        dump("pT", Z[:, :, :], [128, 8, 2048], BF16, XK("Z", R8, range(4)))
        if stop_after == "Y1":
            S.emit(nc, dump_ops)
            return nc

        s_q, k_q = ring_use()
        s_k, k_k = ring_use()
        s_v0, k_v0 = ring_use()
        s_v1, k_v1 = ring_use()
        qkT = [scr[2][:, 0:512].bitcast(BF16), scr[2][:, 512:1024].bitcast(BF16)]
        qh_b = [scr[3][:, 0:256].bitcast(BF16), scr[3][:, 512:768].bitcast(BF16)]
        kh_b = [scr[3][:, 256:512].bitcast(BF16), scr[3][:, 768:1024].bitcast(BF16)]
        P_b = [scr[4][:, 0:256].bitcast(BF16), scr[4][:, 256:512].bitcast(BF16)]
        gnw = S_t
        cload(gnw, gnw_d, ("AR", 2))
        gam = [1.0 - 2.0 ** (-5.0 - h) for h in range(4)]
        cdec = [float(g ** 128) for g in gam]
        qdec3 = qdec[:, :].unsqueeze(2).to_broadcast([128, 4, 128])
        kdec3 = kdec[:, :].unsqueeze(2).to_broadcast([128, 4, 128])

        def R_Ap(t, which):
            lst = {"qk": ((s_q, k_q, 0), (s_k, k_k, 1)), "v0": ((s_v0, k_v0, 2),), "v1": ((s_v1, k_v1, 3),)}[which]
            for (slot, skey, b) in lst:
                for k in R8:
                    mm(pa[b][:, :], ("pa", b), X[:, k, t * 128:(t + 1) * 128], [("X", k, t // 4)], slot[:, k, :], [skey], k == 0, k == 7)

        def R_Av(t, which):
            p2 = t % 2
            if which == "v0":
                S.op("act", lambda e: e.copy(out=vt_b[p2][:, 0:512], in_=pa[2][:, :]), reads=[("pa", 2)], writes=[("vt", p2)])
            else:
                S.op("act", lambda e: e.copy(out=vt_b[p2][:, 512:1024], in_=pa[3][:, :]), reads=[("pa", 3)], writes=[("vt", p2)])

        r_tails = {}

        def R_A3(t):
            tq, tk = r_tails.pop(t)
            tq()
            tk()

        def R_A2(t):
            p2 = t % 2

            def decq(Rr, rkey):
                S.op("pool", lambda e: e.tensor_tensor(out=qh_b[p2].rearrange("p (h d) -> p h d", h=4),
                                                       in0=Rr.rearrange("p (h d) -> p h d", h=4), in1=qdec3, op=ALU.mult),
                     reads=[rkey, "qdec"], writes=[SH(3, p2)])

            def deck(Rr, rkey):
                S.op("pool", lambda e: e.tensor_tensor(out=kh_b[p2].rearrange("p (h d) -> p h d", h=4),
                                                       in0=Rr.rearrange("p (h d) -> p h d", h=4), in1=kdec3, op=ALU.mult),
                     reads=[rkey, "kdec"], writes=[SH(3, p2)])
            tq = rotary(0, t, decq, 0, defer="dec")
            tk = rotary(1, t, deck, 1, defer="dec")
            r_tails[t] = (tq, tk)

        def R_B(t):
            p2 = t % 2
            pi = ctr["pt"] % 2
            ctr["pt"] += 1
            for h in range(4):
                S.op("pe", lambda e, h=h: e.transpose(pt[pi][:, h * 128:(h + 1) * 128], qh_b[p2][:, h * 128:(h + 1) * 128], ident[:, :]),
                     reads=[SH(3, p2), "ident"], writes=[("pt", pi)])
            for h in range(4):
                S.op("pe", lambda e, h=h: e.transpose(pt[pi][:, 512 + h * 128:512 + (h + 1) * 128], kh_b[p2][:, h * 128:(h + 1) * 128], ident[:, :]),
                     reads=[SH(3, p2), "ident"], writes=[("pt", pi)])
            S.op("act", lambda e: e.copy(out=qkT[p2], in_=pt[pi][:, :]), reads=[("pt", pi)], writes=[SH(2, p2)])

        def R_C(t):
            p2 = t % 2
            b = 4
            for h in range(4):
                mm(pa[b][:, h * 128:(h + 1) * 128], ("pa", b), qkT[p2][:, 512 + h * 128:512 + (h + 1) * 128], [SH(2, p2)],
                   qkT[p2][:, h * 128:(h + 1) * 128], [SH(2, p2)], True, True)
            S.op("dve", lambda e: e.tensor_tensor(out=P_b[p2], in0=pa[b][:, :], in1=mask[:, :], op=ALU.mult),
                 reads=[("pa", b), "mask"], writes=[SH(4, 0)])

        def R_D(t):
            p2 = t % 2
            bo = [5, 4]
            bd = [0, 1]
            on = scr[p2]
            for h in range(4):
                d_ap = pa[bd[h // 2]][:, (h % 2) * 256:(h % 2 + 1) * 256]
                mm(d_ap, ("pa", bd[h // 2]), kh_b[p2][:, h * 128:(h + 1) * 128], [SH(3, p2)],
                   vt_b[p2][:, h * 256:(h + 1) * 256], [("vt", p2)], True, True)
            for h in range(4):
                o_ap = pa[bo[h // 2]][:, (h % 2) * 256:(h % 2 + 1) * 256]
                mm(o_ap, ("pa", bo[h // 2]), P_b[p2][:, h * 128:(h + 1) * 128], [SH(4, 0)],
                   vt_b[p2][:, h * 256:(h + 1) * 256], [("vt", p2)], True, False)
                mm(o_ap, ("pa", bo[h // 2]), qkT[p2][:, h * 128:(h + 1) * 128], [SH(2, p2)],
                   state_bf[:, h * 256:(h + 1) * 256], [("state_bf", h)], False, True)
                if h % 2 == 1:
                    hb = h // 2
                    S.op("act", lambda e, hb=hb: e.copy(out=on[:, hb * 512:(hb + 1) * 512], in_=pa[bo[hb]][:, :]),
                         reads=[("pa", bo[hb])], writes=[SH(p2, hb)])
            for h in range(4):
                d_ap = pa[bd[h // 2]][:, (h % 2) * 256:(h % 2 + 1) * 256]
                st_ap = state[:, h * 256:(h + 1) * 256]
                S.op("dve", lambda e, h=h, d_ap=d_ap, st_ap=st_ap: e.scalar_tensor_tensor(out=st_ap, in0=st_ap, scalar=cdec[h], in1=d_ap,
                                                                                        op0=ALU.mult, op1=ALU.add),
                     reads=[("pa", bd[h // 2]), ("state", h)], writes=[("state", h)])
                if h % 2 == 1:
                    hb = h // 2
                    S.op("act", lambda e, hb=hb: e.copy(out=state_bf[:, hb * 512:(hb + 1) * 512], in_=state[:, hb * 512:(hb + 1) * 512]),
                         reads=[("state", 2 * hb), ("state", 2 * hb + 1)], writes=[("state_bf", 2 * hb), ("state_bf", 2 * hb + 1)])
            for h in range(4):
                o_sb = on[:, h * 256:(h + 1) * 256]
                S.op("dve", lambda e, h=h, o_sb=o_sb: e.bn_stats(out=bn6[:, p2, h, :], in_=o_sb), reads=[SH(p2, h // 2)], writes=[("bn6", p2, h)])
                S.op("dve", lambda e, h=h: e.bn_aggr(out=mv[:, p2, h, :], in_=bn6[:, p2, h, :]), reads=[("bn6", p2, h)], writes=[("mv", p2)])
            S.op("dve", lambda e: e.tensor_scalar(out=gve[:, p2, :], in0=mv[:, p2, :, 1], scalar1=EPS, scalar2=None, op0=ALU.add),
                 reads=[("mv", p2)], writes=[("gve", p2)])
            S.op("pool", lambda e: e.tensor_tensor(out=grs[:, p2, :], in0=gve[:, p2, :], in1=mhalf[:, :], op=ALU.pow),
                 reads=[("gve", p2), "mhalf"], writes=[("grs", p2)])
            S.op("dve", lambda e: e.scalar_tensor_tensor(out=gnm[:, p2, :], in0=mv[:, p2, :, 0], scalar=-1.0, in1=grs[:, p2, :],
                                                         op0=ALU.mult, op1=ALU.mult),
                 reads=[("mv", p2), ("grs", p2)], writes=[("gnm", p2)])
        def R_D2(t):
            p2 = t % 2
            on = scr[p2]
            for h in range(4):
                o_sb = on[:, h * 256:(h + 1) * 256]
                S.op("act", lambda e, h=h, o_sb=o_sb: e.activation(out=o_sb, in_=o_sb, func=AF.Identity,
                                                                   scale=grs[:, p2, h:h + 1], bias=gnm[:, p2, h:h + 1]),
                     reads=[SH(p2, h // 2), ("grs", p2), ("gnm", p2)], writes=[SH(p2, h // 2)])
            x3 = t % 3
            for hb in range(2):
                S.op("pool", lambda e, hb=hb: e.tensor_tensor(out=xs_b[x3][:, hb * 512:(hb + 1) * 512], in0=on[:, hb * 512:(hb + 1) * 512],
                                                             in1=gnw[:, hb * 512:(hb + 1) * 512], op=ALU.mult),
                     reads=[SH(p2, hb), ("AR", 2)], writes=[("xs", x3)])

        def R_E(t):
            x3 = t % 3
            pi = transpose8(xs_b[x3], ("xs", x3))
            S.op("act", lambda e: e.copy(out=Y[:, :, t * 128:(t + 1) * 128], in_=pt[pi][:, :].rearrange("p (k m) -> p k m", k=8)),
                 reads=[("pt", pi)], writes=XK("Y", R8, [t // 4]))

        R_Ap(0, "qk")
        R_A2(0)
        R_Ap(0, "v0")
        R_Av(0, "v0")
        R_Ap(0, "v1")
        R_Av(0, "v1")
        R_A3(0)
        for i in range(16):
            nx = i + 1 < 16
            if nx:
                R_Ap(i + 1, "qk")
                R_A2(i + 1)
            R_B(i)
            if nx:
                R_Ap(i + 1, "v0")
                R_Av(i + 1, "v0")
            if i >= 2:
                R_E(i - 2)
            R_C(i)
            if nx:
                R_Ap(i + 1, "v1")
                R_Av(i + 1, "v1")
                R_A3(i + 1)
                if i + 1 == 15:
                    ring_rel(4)
            if i >= 1:
                R_D2(i - 1)
            R_D(i)
        R_D2(15)
        r2_groups = [0]

        for jg in range(2):
            s_g, k_g = ring_use()
            for jj in range(4):
                j = jg * 4 + jj
                for tb in range(4):
                    bg = nbank()
                    for k in R8:
                        mm(pa[bg][:, :], ("pa", bg), s_g[:, k, jj * 128:(jj + 1) * 128], [k_g],
                           X[:, k, tb * 512:(tb + 1) * 512], [("X", k, tb)], k == 0, k == 7)
                    si = tb % 2
                    sg = scr[0][:, si * 512:(si + 1) * 512]
                    S.op("act", lambda e, sg=sg, bg=bg: e.activation(out=sg, in_=pa[bg][:, :], func=AF.Silu),
                         reads=[("pa", bg)], writes=[SH(0, si)])
                    dst = Y[:, j, tb * 512:(tb + 1) * 512]
                    S.op("pool", lambda e, sg=sg, dst=dst: e.tensor_tensor(out=dst, in0=sg, in1=dst, op=ALU.mult),
                         reads=[SH(0, si), ("Y", j, tb)], writes=[("Y", j, tb)])
                    if r2_groups[0] == 0:
                        R_E(14)
                    elif r2_groups[0] == 1:
                        R_E(15)
                    r2_groups[0] += 1
            ring_rel(1)
        dump("zretT", Y[:, :, :], [128, 8, 2048], BF16, XK("Y", R8, range(4)))
        if stop_after == "R":
            S.emit(nc, dump_ops)
            return nc

        make_SB(4, 3, n2w_d)
        phase_Y(Y, "Y", True)
        dump("mergedT", Z[:, :, :], [128, 8, 2048], BF16, XK("Z", R8, range(4)))

        s_o0, k_o0 = ring_use()
        s_o1, k_o1 = ring_use()
        for n, (slot, skey) in enumerate(((s_o0, k_o0), (s_o1, k_o1))):
            gb = G_t[:, n * 512:(n + 1) * 512].unsqueeze(1).to_broadcast([128, 4, 512])
            for kh in range(2):
                S.op("dve" if kh == 0 else "pool",
                     lambda e, slot=slot, kh=kh, gb=gb: e.tensor_tensor(out=slot[:, kh * 4:(kh + 1) * 4, :], in0=slot[:, kh * 4:(kh + 1) * 4, :],
                                                                       in1=gb, op=ALU.mult),
                     reads=[skey, "G"], writes=[skey])
        pend = {}

        o_x = {}
        o_h = {}

        def O_load(t):
            o_x[t] = load_x(x_main, t)

        def O_mm(t):
            xi = o_x[t]
            b0, b1 = nbank(), nbank()
            for (slot, skey, b) in ((s_o0, k_o0, b0), (s_o1, k_o1, b1)):
                for k in R8:
                    mm(pa[b][:, :], ("pa", b), Z[:, k, t * 128:(t + 1) * 128], [("Z", k, t // 4)], slot[:, k, :], [skey], k == 0, k == 7)
            for n, b in ((0, b0), (1, b1)):
                S.op("dve", lambda e, n=n, b=b: e.tensor_tensor(out=scr[xi][:, n * 512:(n + 1) * 512], in0=pa[b][:, :],
                                                               in1=scr[xi][:, n * 512:(n + 1) * 512], op=ALU.add),
                     reads=[("pa", b), SH(xi, n)], writes=[SH(xi, n)])
            dma("act", h1s[t * 128:(t + 1) * 128, :], scr[xi][:, :], SK(xi), [("h1s", t)], ("h1st", t % 2))
            o_h[t] = nA(scr[xi][:, :], SK(xi))

        def O_T(t):
            xsi = pend[t]
            pi = transpose8(xs_b[xsi], ("xs", xsi))
            S.op("act", lambda e: e.copy(out=X[:, :, t * 128:(t + 1) * 128], in_=pt[pi][:, :].rearrange("p (k m) -> p k m", k=8)),
                 reads=[("pt", pi)], writes=XK("X", R8, [t // 4]))

        def O_iter(i):
            if 2 <= i < 18:
                pend[i - 2] = nC(o_h[i - 2], add_eng=("pool" if i % 2 == 0 else "dve"))
            if i < 16:
                O_mm(i)
            if 1 <= i < 17:
                nB(o_h[i - 1])
            if i + 1 < 16:
                O_load(i + 1)
            if 3 <= i < 19:
                O_T(i - 3)

        O_load(0)
        for i in range(16):
            O_iter(i)
        ring_rel(2)
        O_iter(16)

        modvec(5, G_t, ["G"])
        O_iter(17)
        fnw = xsT[:, 0:2048].bitcast(F32)
        f_groups = [0]
        h2a = AR[:, :].rearrange("p (t n) -> p t n", n=512)
        out_ops = []

        def hid_ap(f, c0, c1):
            buf = Y if f < 16 else Z
            ff = f % 16
            return buf[:, ff // 2, (ff % 2) * 1024 + c0:(ff % 2) * 1024 + c1]

        def hid_keys(f, tb2s):
            name = "Y" if f < 16 else "Z"
            ff = f % 16
            return [(name, ff // 2, (ff % 2) * 2 + tb2) for tb2 in tb2s]

        for mb in range(2):
            for fc in range(8):
                s_w, k_w = ring_use()
                for ff in range(4):
                    f = fc * 4 + ff
                    for tb2 in range(2):
                        tb = mb * 2 + tb2
                        b = nbank()
                        for k in R8:
                            mm(pa[b][:, :], ("pa", b), s_w[:, k, ff * 128:(ff + 1) * 128], [k_w],
                               X[:, k, tb * 512:(tb + 1) * 512], [("X", k, tb)], k == 0, k == 7)
                        ri = tb2
                        r = scr[6][:, ri * 512:(ri + 1) * 512]
                        S.op("act", lambda e, r=r, b=b: e.activation(out=r, in_=pa[b][:, :], func=AF.Relu),
                             reads=[("pa", b)], writes=[SH(6, ri)])
                        dst = hid_ap(f, tb2 * 512, (tb2 + 1) * 512)
                        S.op("pool", lambda e, r=r, dst=dst: e.tensor_tensor(out=dst, in0=r, in1=r, op=ALU.mult),
                             reads=[SH(6, ri)], writes=hid_keys(f, [tb2]))
                        f_groups[0] += 1
                        if f_groups[0] == 2:
                            O_iter(18)
                            dma("sp", fnw, fnw_d, [], ["fnw", ("xs", 0), ("xs", 1)], "fnw")
                ring_rel(1)
            for n in range(2):
                sl = [ring_use() for _ in range(4)]
                for tt in range(8):
                    t = mb * 8 + tt
                    b = nbank()
                    for f in range(32):
                        slot, skey = sl[f // 8]
                        mm(pa[b][:, :], ("pa", b), hid_ap(f, tt * 128, (tt + 1) * 128), hid_keys(f, [tt // 4]),
                           slot[:, f % 8, :], [skey], f == 0, f == 31)
                        if tt == 7 and f % 8 == 7:
                            ring_rel(1)
                    xi = load_x(h1s, t, reads=[("h1s", t)])
                    tmpb = scr[5][:, n * 512:(n + 1) * 512]
                    S.op("dve", lambda e, b=b, tmpb=tmpb, n=n: e.tensor_tensor(out=tmpb, in0=pa[b][:, :], in1=G_t[:, n * 512:(n + 1) * 512], op=ALU.mult),
                         reads=[("pa", b), "G"], writes=[SH(5, n)])
                    if n == 0:
                        S.op("pool", lambda e, tmpb=tmpb, xi=xi, tt=tt: e.tensor_tensor(out=h2a[:, tt, :], in0=tmpb, in1=scr[xi][:, 0:512], op=ALU.add),
                             reads=[SH(5, 0), SH(xi, 0)], writes=[("AR", tt // 2)])
                        continue
                    S.op("pool", lambda e, tmpb=tmpb, xi=xi: e.tensor_tensor(out=scr[xi][:, 512:1024], in0=tmpb, in1=scr[xi][:, 512:1024], op=ALU.add),
                         reads=[SH(5, 1), SH(xi, 1)], writes=[SH(xi, 1)])
                    S.op("pool", lambda e, xi=xi, tt=tt: e.tensor_copy(out=scr[xi][:, 0:512], in_=h2a[:, tt, :]),
                         reads=[("AR", tt // 2), SH(xi, 0)], writes=[SH(xi, 0)])
                    r_ap, rkey = norm_stats(scr[xi][:, :], SK(xi))
                    ctr["tmp"] += 1
                    oi = 3 + (ctr["tmp"] % 2)
                    S.op("dve", lambda e, xi=xi, oi=oi, r_ap=r_ap: e.scalar_tensor_tensor(out=scr[oi][:, :], in0=scr[xi][:, :], scalar=r_ap, in1=fnw,
                                                                                         op0=ALU.mult, op1=ALU.mult),
                         reads=SK(xi) + [rkey, "fnw"], writes=SK(oi))
                    out_ops.append(dma("act", y_out[t * 128:(t + 1) * 128, :], scr[oi][:, :], SK(oi), [("yout", t)], ("yst", t % 2)))

        S.emit(nc, list(out_ops) + dump_ops)
    return nc


def _consts(half):
    gam = np.array([1.0 - 2.0 ** (-5.0 - h) for h in range(4)], np.float64)
    s = np.arange(128)[:, None]
    c = np.arange(128)[None, :]
    same = (s // 64) == (c // 64)
    mask = np.zeros((128, 4, 128), np.float64)
    qdec = np.zeros((128, 4), np.float64)
    kdec = np.zeros((128, 4), np.float64)
    predec = np.zeros((128, 16, 4), np.float64)
    for h in range(4):
        g = gam[h]
        Dm = np.where(same, g ** np.abs(c - s), np.where(s < c, g ** (c - s).clip(0), 0.0))
        mask[:, h, :] = Dm * g ** (s - c - 128.0)
        qdec[:, h] = g ** (np.arange(128) + 1.0)
        kdec[:, h] = g ** (127.0 - np.arange(128)) * 128.0 ** -0.5
        tok = np.arange(16)[None, :] * 128 + np.arange(128)[:, None]
        predec[:, :, h] = (g ** (2047.0 - tok)) * 128.0 ** -0.5 * float(half)
    invf = (10000.0 ** (-np.arange(0, 128, 2, dtype=np.float32) / 128)).astype(np.float32)
    return dict(
        mask=mask.reshape(128, 512).astype(np.float32),
        qdec_rep=qdec.astype(np.float32),
        kdec_rep=kdec.astype(np.float32),
        predec=predec.reshape(128, 64).astype(np.float32),
        invf=np.ascontiguousarray(np.broadcast_to(invf[None, :], (128, 64))),
        flagcol=np.full((128, 1), float(half), np.float32),
        ident=np.eye(128, dtype=np.float32).astype(ml_dtypes.bfloat16),
    )


def make_in_maps(inputs):
    f32 = lambda a: np.ascontiguousarray(np.asarray(a, dtype=np.float32))
    x = f32(inputs["x"])
    c = f32(inputs["c"])
    pos = np.ascontiguousarray(np.asarray(inputs["positions"], dtype=np.int32))
    rep = lambda v: np.ascontiguousarray(np.broadcast_to(f32(v).reshape(1, -1), (128, f32(v).size)))
    shared = dict(
        w_ada=f32(inputs["w_ada"][0]), b_ada_rep=rep(inputs["b_ada"][0]),
        n1w_rep=rep(inputs["norm1_w"][0]), n2w_rep=rep(inputs["norm2_w"][0]),
        fnw_rep=rep(inputs["final_norm_w"]), gnw_rep=rep(inputs["ret_gn_w"][0]),
        w_in=f32(inputs["w_in"][0]), w_ro=f32(inputs["w_ret_out"][0]), w_co=f32(inputs["w_conv_out"][0]),
        w_o=f32(inputs["w_o"][0]), w_ff1=f32(inputs["w_ff1"][0]), w_ff2=f32(inputs["w_ff2"][0]),
        convw=np.ascontiguousarray(f32(inputs["conv_w"][0]).reshape(3, 8, 128).transpose(2, 1, 0).reshape(128, 24)),
    )
    maps = []
    for core in range(8):
        b, half = core // 2, core % 2
        m = dict(shared)
        m.update(_consts(half))
        m["x_main"] = np.ascontiguousarray(x[b, half * 2048:(half + 1) * 2048])
        m["x_pre"] = np.ascontiguousarray(x[b, 0:2048])
        m["cT"] = np.ascontiguousarray(c[b].reshape(8, 128).T)
        m["pos_m"] = np.ascontiguousarray(pos[b, half * 2048:(half + 1) * 2048].reshape(16, 128).T)
        m["pos_p"] = np.ascontiguousarray(pos[b, 0:2048].reshape(16, 128).T)
        maps.append(m)
    return maps


def kernel(**inputs):
    nc = build_program()
    maps = make_in_maps(inputs)
    res = run_bass_kernel_spmd(nc, maps, core_ids=list(range(8)))
    out = np.empty((4, 4096, 1024), np.float32)
    for core in range(8):
        b, half = core // 2, core % 2
        out[b, half * 2048:(half + 1) * 2048] = np.asarray(res.results[core]["y"], dtype=np.float32)
    return out
```
